# Optimizing a Trainium2 kernel written in Bass

```python
import math
import jax, jax.numpy as jnp
from jax import lax
import numpy as np

D_MODEL = 1024
BATCH = 8
SEQ = 2048
DEPTH = 4
DEC_BATCH = 128
DEC_SEQ = 1
PAST_LEN = 16384
PAGE_SIZE = 128

N_MIXERS = 2
N_A_LAYERS = (DEPTH + 1) // 2
N_B_LAYERS = DEPTH // 2
DK_A = 128
H_A = D_MODEL // DK_A
DV_A = D_MODEL // H_A
CHUNK_A = 32
CHUNK_B = 128
D_INNER_B = D_MODEL
G_B = 8
DG_B = D_INNER_B // G_B
D_FF = -(-(8 * D_MODEL) // (3 * 256)) * 256
ALPHA = (2 * DEPTH) ** 0.25
BETA = (8 * DEPTH) ** -0.25
LN_EPS = 1e-5
RMS_EPS = 1e-6

kernel_name = "hgrn2_chunkmlp_hybrid_step"


def layer_norm(x, g, b):
    xf = x.astype(jnp.float32)
    mu = jnp.mean(xf, axis=-1, keepdims=True)
    var = jnp.mean(jnp.square(xf - mu), axis=-1, keepdims=True)
    return ((xf - mu) * lax.rsqrt(var + LN_EPS) * g + b).astype(x.dtype)


def rms_norm(x, g):
    xf = x.astype(jnp.float32)
    return xf * lax.rsqrt(jnp.mean(jnp.square(xf), axis=-1, keepdims=True) + RMS_EPS) * g


def hgrn2_recurrence(q, k, v, logf, s0):
    b, L, h, dk = q.shape
    dv = v.shape[-1]
    c = math.gcd(L, CHUNK_A)
    n = L // c

    def to_chunks(a):
        return a.reshape(b, n, c, h, a.shape[-1]).transpose(1, 0, 3, 2, 4)

    causal = jnp.tril(jnp.ones((c, c), dtype=bool))[:, :, None]

    def step(s, blk):
        qb, kb, vb, gb = blk
        g = jnp.cumsum(gb, axis=2)
        o_inter = jnp.einsum("bhtk,bhkv->bhtv", qb * jnp.exp(g), s)
        diff = g[:, :, :, None, :] - g[:, :, None, :, :]
        decay = jnp.exp(jnp.where(causal, diff, -jnp.inf))
        a = jnp.einsum("bhtk,bhsk,bhtsk->bhts", qb, kb, decay)
        o_intra = jnp.einsum("bhts,bhsv->bhtv", a, vb)
        g_end = g[:, :, -1:, :]
        s_new = jnp.exp(g_end[:, :, 0, :, None]) * s + jnp.einsum(
            "bhsk,bhsv->bhkv", kb * jnp.exp(g_end - g), vb)
        return s_new, o_inter + o_intra

    s_fin, o = lax.scan(step, s0, (to_chunks(q), to_chunks(k), to_chunks(v), to_chunks(logf)))
    o = o.transpose(1, 0, 3, 2, 4).reshape(b, L, h, dv)
    return o, s_fin


def hgrn2_mixer(x, lower_bound, w_in, norm_g, w_out, s0):
    b, L, _ = x.shape
    dq = H_A * DK_A
    proj = (x @ w_in).astype(jnp.float32)
    q_raw, f_raw, i_raw, g_raw = jnp.split(proj, [dq, 2 * dq, 2 * dq + H_A * DV_A], axis=-1)
    q = jax.nn.silu(q_raw).reshape(b, L, H_A, DK_A) * (DK_A ** -0.5)
    f = lower_bound + (1.0 - lower_bound) * jax.nn.sigmoid(f_raw)
    k = (1.0 - f).reshape(b, L, H_A, DK_A)
    logf = jnp.log(f).reshape(b, L, H_A, DK_A)
    v = i_raw.reshape(b, L, H_A, DV_A)
    o, s_fin = hgrn2_recurrence(q, k, v, logf, s0.astype(jnp.float32))
    o = rms_norm(o, norm_g) * jax.nn.silu(g_raw).reshape(b, L, H_A, DV_A)
    y = o.reshape(b, L, H_A * DV_A).astype(x.dtype) @ w_out
    return y, s_fin


def chunk_mlp(x, w_in, ln_g, ln_b, w_s, b_s, w_out):
    b, L, _ = x.shape
    hdn = jax.nn.gelu(x @ w_in, approximate=False)
    u, v = jnp.split(hdn, 2, axis=-1)
    v = layer_norm(v, ln_g, ln_b)
    c = min(L, CHUNK_B)
    n = L // c
    w = jnp.tril(w_s[:, :c, :c])
    bias = b_s[:, :c].T[None, None, :, :, None]
    mixed = jnp.einsum("gts,bnsgd->bntgd", w, v.reshape(b, n, c, G_B, DG_B)) + bias
    y = (u * mixed.reshape(b, L, D_INNER_B)).astype(x.dtype) @ w_out
    return y, v[:, L - c:]


def swiglu_ffn(x, w_in, w_out):
    gate, up = jnp.split(x @ w_in, 2, axis=-1)
    return (jax.nn.silu(gate) * up) @ w_out


def setup_inputs(seed: int = 0) -> dict:
    key = jax.random.key(seed)
    ks = jax.random.split(key, 19)
    f32 = jnp.float32

    def nrm(k, shape, scale):
        return jax.random.normal(k, shape, f32) * scale

    return {
        "x_prompt": nrm(ks[0], (BATCH, SEQ, D_MODEL), 1.0),
        "x_sample": nrm(ks[1], (DEC_BATCH, DEC_SEQ, D_MODEL), 1.0),
        "state_hgrn": nrm(ks[2], (N_A_LAYERS, DEC_BATCH, H_A, DK_A, DV_A), 0.5),
        "ln_mix_g": 1.0 + nrm(ks[3], (DEPTH, D_MODEL), 0.05),
        "ln_mix_b": nrm(ks[4], (DEPTH, D_MODEL), 0.05),
        "ln_ffn_g": 1.0 + nrm(ks[5], (DEPTH, D_MODEL), 0.05),
        "ln_ffn_b": nrm(ks[6], (DEPTH, D_MODEL), 0.05),
        "a_lb_raw": nrm(ks[7], (DEPTH, H_A * DK_A), 0.5),
        "a_w_in": nrm(ks[8], (N_A_LAYERS, D_MODEL, 2 * H_A * DK_A + 2 * H_A * DV_A), D_MODEL ** -0.5),
        "a_norm_g": 1.0 + nrm(ks[9], (N_A_LAYERS, DV_A), 0.05),
        "a_w_out": nrm(ks[10], (N_A_LAYERS, H_A * DV_A, D_MODEL), BETA * (H_A * DV_A) ** -0.5),
        "b_w_in": nrm(ks[11], (N_B_LAYERS, D_MODEL, 2 * D_INNER_B), D_MODEL ** -0.5),
        "b_ln_g": 1.0 + nrm(ks[12], (N_B_LAYERS, D_INNER_B), 0.05),
        "b_ln_b": nrm(ks[13], (N_B_LAYERS, D_INNER_B), 0.05),
        "b_w_s": nrm(ks[14], (N_B_LAYERS, G_B, CHUNK_B, CHUNK_B), CHUNK_B ** -0.5),
        "b_bias_s": 1.0 + nrm(ks[15], (N_B_LAYERS, G_B, CHUNK_B), 0.1),
        "b_w_out": nrm(ks[16], (N_B_LAYERS, D_INNER_B, D_MODEL), BETA * D_INNER_B ** -0.5),
        "ffn_w_in": nrm(ks[17], (DEPTH, D_MODEL, 2 * D_FF), D_MODEL ** -0.5),
        "ffn_w_out": nrm(ks[18], (DEPTH, D_FF, D_MODEL), BETA * D_FF ** -0.5),
    }


def reference(x_prompt, x_sample, state_hgrn, ln_mix_g, ln_mix_b, ln_ffn_g, ln_ffn_b,
              a_lb_raw, a_w_in, a_norm_g, a_w_out, b_w_in, b_ln_g, b_ln_b, b_w_s, b_bias_s,
              b_w_out, ffn_w_in, ffn_w_out):
    p = jax.nn.softmax(a_lb_raw.astype(jnp.float32), axis=0)
    lower_bounds = jnp.cumsum(p, axis=0) - p[0]

    def trunk(x, hgrn_init):
        hgrn_out, v_out = [], []
        for layer in range(DEPTH):
            j = layer // N_MIXERS
            if layer % N_MIXERS == 0:
                h, s = hgrn2_mixer(x, lower_bounds[layer], a_w_in[j], a_norm_g[j], a_w_out[j], hgrn_init[j])
                hgrn_out.append(s.astype(state_hgrn.dtype))
            else:
                h, v_rows = chunk_mlp(x, b_w_in[j], b_ln_g[j], b_ln_b[j], b_w_s[j], b_bias_s[j], b_w_out[j])
                v_out.append(v_rows)
            x = layer_norm(ALPHA * x + h, ln_mix_g[layer], ln_mix_b[layer])
            x = layer_norm(ALPHA * x + swiglu_ffn(x, ffn_w_in[layer], ffn_w_out[layer]),
                           ln_ffn_g[layer], ln_ffn_b[layer])
        return x, jnp.stack(hgrn_out), jnp.stack(v_out)

    prompt_init = jnp.zeros((N_A_LAYERS, x_prompt.shape[0], H_A, DK_A, DV_A), jnp.float32)
    y_prompt, hgrn_state_prompt, chunk_v_prompt = trunk(x_prompt, prompt_init)
    y_sample, hgrn_state_sample, chunk_v_sample = trunk(x_sample, state_hgrn)
    return (y_prompt, y_sample, hgrn_state_prompt, hgrn_state_sample, chunk_v_prompt, chunk_v_sample)
```

```python
import numpy as np
import concourse.bass as bass
import concourse.mybir as mybir
from concourse.bass_utils import run_bass_kernel_spmd

F32 = mybir.dt.float32
BF16 = mybir.dt.bfloat16
AF = mybir.ActivationFunctionType
ALU = mybir.AluOpType

NCORES = 8
D = 1024
SEQ = 2048
NS = 16
NT = SEQ + NS
NTILE = 17
DEPTH = 4
DFF = 2816
NFC = DFF // 128
ALPHA = (2 * DEPTH) ** 0.25
LN_EPS = 1e-5
RMS_EPS = 1e-6
QSCALE = 128 ** -0.5
SEM_ROT = 1 << 30


class Op:
    __slots__ = ("eng", "fn", "deps", "signal", "sem", "val", "is_dma", "idx", "clear")


def _ap_range(ap):
    t = ap.tensor
    dims = list(ap.ap)
    rowlen = dims[0][0]
    off = int(ap.offset)
    if rowlen <= 0:
        rowlen = 1 << 40
    plo = off // rowlen
    clo = off % rowlen
    phi = plo + dims[0][1]
    lo = clo
    hi = clo
    for st, cnt in dims[1:]:
        if st >= 0:
            hi += st * (cnt - 1)
        else:
            lo += st * (cnt - 1)
    return (t.name, plo, phi, lo, hi + 1)


class Sched:
    ENG = ("pe", "act", "dve", "pool", "sp")

    def __init__(self, nc):
        self.nc = nc
        self.ops = {e: [] for e in self.ENG}
        self.recs = {}
        self.nsig = {e: 0 for e in self.ENG}
        self.ndma = 0
        self.dma_last = {}
        self.NDMASEM = 16
        self.psum_bank = {"ps": 512, "pt": 1024}
        self.NSW = 8
        self.nsw = 0
        self.sw_pending = {}

    def add(self, eng, fn, reads=(), writes=(), dma=False, extra=(), sw=False, clear=()):
        op = Op()
        op.clear = list(clear)
        op.eng = eng
        op.fn = fn
        op.signal = False
        op.sem = None
        op.val = None
        op.is_dma = dma
        op.idx = len(self.ops[eng])
        deps = set(extra)
        for ap, isw in [(a, False) for a in reads] + [(a, True) for a in writes]:
            name, plo, phi, lo, hi = _ap_range(ap)
            bank = self.psum_bank.get(name)
            if bank is not None and isw and eng == "pe":
                lo = (lo // bank) * bank
                hi = -(-hi // bank) * bank
                plo, phi = 0, 128
            lst = self.recs.setdefault(name, [])
            keep = []
            for r in lst:
                ov = not (r[1] <= plo or phi <= r[0] or r[3] <= lo or hi <= r[2])
                if ov and (isw or r[5]):
                    if r[4] is not op:
                        deps.add(r[4])
                cov = plo <= r[0] and r[1] <= phi and lo <= r[2] and r[3] <= hi
                if cov and (isw or (not r[5] and r[4].eng == eng and not r[4].is_dma and not dma)):
                    continue
                keep.append(r)
            keep.append([plo, phi, lo, hi, op, isw])
            self.recs[name] = keep
        if dma and sw:
            op.sem = ("sw", self.nsw)
            self.nsw += 1
            op.val = 16
            op.signal = True
        elif dma:
            slot = self.ndma % self.NDMASEM
            self.ndma += 1
            prev = self.dma_last.get(slot)
            if prev is not None:
                deps.add(prev)
            self.dma_last[slot] = op
            op.sem = ("dma", slot)
            op.val = 16 * ((self.ndma - 1) // self.NDMASEM + 1)
            op.signal = True
        op.deps = [d for d in deps if not (d.eng == "pe" and eng == "pe" and not d.is_dma and not dma)]
        for d in op.deps:
            d.signal = True
        for key in op.clear:
            self.sw_pending.pop(key[1], None)
        self.ops[eng].append(op)
        return op


def run_sched(nc, S, final_ops, ctx_enter):
    last_ops = []
    for e in S.ENG:
        comp = [op for op in S.ops[e] if not op.is_dma]
        if comp:
            comp[-1].signal = True
            last_ops.append(comp[-1])
    sem_objs = {}
    for e in S.ENG:
        n = 0
        for op in S.ops[e]:
            if op.is_dma or not op.signal:
                continue
            key = (e, n // SEM_ROT)
            op.sem = key
            op.val = n % SEM_ROT + 1
            n += 1
            sem_objs.setdefault(key, None)
    for slot in range(min(S.ndma, S.NDMASEM)):
        sem_objs[("dma", slot)] = None
    for slot in range(S.nsw):
        sem_objs[("sw", slot)] = None
    for key in list(sem_objs):
        sem_objs[key] = ctx_enter(nc.semaphore("s_%s_%d" % key))
    def emit_engine(ename, eng):
        waited = {}
        for op in S.ops[ename]:
            best = {}
            for d in op.deps:
                if d.val is None:
                    continue
                if best.get(d.sem, 0) < d.val:
                    best[d.sem] = d.val
            for key, v in best.items():
                if waited.get(key, 0) >= v:
                    continue
                eng.wait_ge(sem_objs[key], v)
                waited[key] = v
                if DEBUG.get('trace'):
                    print('   ', ename, 'WAIT', key, v)
            for key in op.clear:
                eng.sem_clear(sem_objs[key])
                waited.pop(key, None)
            ins = op.fn(eng)
            if DEBUG.get('trace'):
                print(ename, op.idx, 'sig' if op.signal else '', op.sem, op.val, 'dma' if op.is_dma else '')
            if op.is_dma:
                ins.then_inc(sem_objs[op.sem], 16)
            elif op.signal:
                ins.then_inc(sem_objs[op.sem], 1)
        if ename == "sp":
            for slot, op in S.dma_last.items():
                eng.wait_ge(sem_objs[op.sem], op.val)
            for slot in range(S.nsw):
                eng.wait_ge(sem_objs[("sw", slot)], 16)
            for op in last_ops:
                eng.wait_ge(sem_objs[op.sem], op.val)

    block = ctx_enter(nc.Block())

    @block.tensor
    def _(eng):
        emit_engine("pe", eng)

    @block.scalar
    def _(eng):
        emit_engine("act", eng)

    @block.vector
    def _(eng):
        emit_engine("dve", eng)

    @block.gpsimd
    def _(eng):
        emit_engine("pool", eng)

    @block.sync
    def _(eng):
        emit_engine("sp", eng)


import contextlib

STAGES = ["A0", "F0", "B1", "F1", "A2", "F2", "B3", "F3"]
FFN_GROUPS = [(0, 2), (2, 4), (6, 4), (10, 4), (14, 4), (18, 4)]
CH = 64
DEBUG = {}


def build_program(nstages=8):
    nc = bass.Bass("TRN2", target_bir_lowering=False)
    es = contextlib.ExitStack()

    def din(name, shape):
        return nc.dram_tensor(name, list(shape), F32, kind="ExternalInput").ap()

    def dout(name, shape):
        return nc.dram_tensor(name, list(shape), F32, kind="ExternalOutput").ap()

    xp = din("xp", [SEQ, D]); xs = din("xs", [NS, D]); st = din("st", [2, NS, 8, 128, 128])
    lnmg = din("lnmg", [4, D]); lnmb = din("lnmb", [4, D]); lnfg = din("lnfg", [4, D]); lnfb = din("lnfb", [4, D])
    lbT = din("lbT", [128, 32])
    a_win = din("a_win", [2, 4, 128, 8192]); a_wout = din("a_wout", [2, 4, 128, 2048]); a_ng = din("a_ng", [2, 128])
    b_win = din("b_win", [2, 128, 8 * 2048]); b_wout = din("b_wout", [2, 128, 8 * 1024])
    b_lng = din("b_lng", [2, D]); b_lnb = din("b_lnb", [2, D]); b_wsT = din("b_wsT", [2, 128, 1024])
    b_bias = din("b_bias", [2, D]); b_w00 = din("b_w00", [2, 8]); b_b0 = din("b_b0", [2, 8])
    f_win = din("f_win", [4, 128, 8 * 2 * DFF]); f_wout = din("f_wout", [4, 128, NFC * 1024])
    cmask = din("cmask", [128, 256]); cident = din("cident", [128, 128])

    yp = dout("yp", [SEQ, D]); ys = dout("ys", [NS, D]); sp_o = dout("sp", [2, 8, 128, 128])
    ss_o = dout("ss", [2, NS, 8, 128, 128]); cvp = dout("cvp", [2, 128, D]); cvs = dout("cvs", [2, NS, D])

    def T(name, shape, dt):
        return es.enter_context(nc.sbuf_tensor(name, list(shape), dt))

    X = T("X", [128, NTILE * D], F32)
    XT = T("XT", [128, 8 * NT], BF16)
    ARENA = T("ARENA", [128, 24576], BF16)
    LNP = T("LNP", [128, 4096], F32)
    CM = T("CM", [128, 256], F32)
    IDF = T("IDF", [128, 128], F32)
    IDB = T("IDB", [128, 128], BF16)
    ONESF = T("ONESF", [128, 128], F32)
    NG = T("NG", [128, 256], F32)
    COEF = T("COEF", [128, 64], F32)
    LBW = T("LBW", [128, 160], F32)
    STAT = T("STAT", [128, 128], F32)
    SML = T("SML", [128, 64], F32)
    SCRF = T("SCRF", [128, 6400], F32)
    SCRB = T("SCRB", [128, 6144], BF16)
    ps = es.enter_context(nc.psum_tensor("ps", [128, 7 * 512], F32))
    pt = es.enter_context(nc.psum_tensor("pt", [128, 1024], BF16))

    X3 = X[:].rearrange("p (i d) -> p i d", d=D)
    XT3 = XT[:].rearrange("p (k t) -> p k t", k=8)
    pt3 = pt[:].rearrange("p (k t) -> p k t", k=8)
    NEGHALF = SML[:, 0:2]
    MASKA = CM[:, 0:128]
    TRILT = CM[:, 128:256]

    S = Sched(nc)

    def A(eng, fn, r=(), w=(), dma=False):
        return S.add(eng, fn, reads=r, writes=w, dma=dma)

    def mm(out, lhsT, rhs, start, stop):
        A("pe", lambda e: e.matmul(out, lhsT=lhsT, rhs=rhs, start=start, stop=stop), r=[lhsT, rhs], w=[out])

    def tr_b(out, in_, nr):
        idn = IDB[0:nr, 0:nr]
        A("pe", lambda e: e.transpose(out=out, in_=in_, identity=idn), r=[in_, idn], w=[out])

    def tr_f(out, in_, nr):
        idn = IDF[0:nr, 0:nr]
        A("pe", lambda e: e.transpose(out=out, in_=in_, identity=idn), r=[in_, idn], w=[out])

    def act(out, in_, func, scale=1.0, accum=None):
        if accum is None:
            A("act", lambda e: e.activation(out=out, in_=in_, func=func, scale=scale), r=[in_], w=[out])
        else:
            A("act", lambda e: e.activation(out=out, in_=in_, func=func, scale=scale, accum_out=accum), r=[in_], w=[out, accum])

    def acopy(out, in_):
        A("act", lambda e: e.copy(out=out, in_=in_), r=[in_], w=[out])

    def tt(eng, out, in0, in1, op):
        A(eng, lambda e: e.tensor_tensor(out=out, in0=in0, in1=in1, op=op), r=[in0, in1], w=[out])

    def ts(eng, out, in0, s1, s2, op0, op1=None):
        rs = [in0] + [s for s in (s1, s2) if not isinstance(s, (int, float)) and s is not None]
        if op1 is None:
            A(eng, lambda e: e.tensor_scalar(out=out, in0=in0, scalar1=s1, scalar2=None, op0=op0), r=rs, w=[out])
        else:
            A(eng, lambda e: e.tensor_scalar(out=out, in0=in0, scalar1=s1, scalar2=s2, op0=op0, op1=op1), r=rs, w=[out])

    def stt(out, in0, scalar, in1, op0, op1):
        rs = [in0, in1] + ([] if isinstance(scalar, (int, float)) else [scalar])
        A("dve", lambda e: e.scalar_tensor_tensor(out=out, in0=in0, scalar=scalar, in1=in1, op0=op0, op1=op1), r=rs, w=[out])

    def vcopy(eng, out, in_):
        A(eng, lambda e: e.tensor_copy(out=out, in_=in_), r=[in_], w=[out])

    def memset(eng, ap, val):
        A(eng, lambda e: e.memset(ap, val), w=[ap])

    def dma_in(out, in_, q="sp"):
        return A(q, lambda e: e.dma_start(out=out, in_=in_), w=[out], dma=True)

    def dma_out(out, in_, q="sp"):
        return A(q, lambda e: e.dma_start(out=out, in_=in_), r=[in_], dma=True)

    def wload(dst, src):
        return S.add("pool", lambda e: e.dma_start(out=dst, in_=src, max_dma_last_dim=4096), writes=[dst], dma=True, sw=True)

    def wrelay():
        pass

    pend = []
    XBS = [SCRB[:, 4096:5120], SCRB[:, 5120:6144]]
    xb_i = [0]

    def xt_convert(items, cast_eng="act"):
        for g0 in range(0, len(items), 2):
            grp = items[g0:g0 + 2]
            bufs = []
            for (i, nr) in grp:
                xb = XBS[xb_i[0] % 2]
                xb_i[0] += 1
                bufs.append(xb)
                if cast_eng == "act":
                    acopy(xb[0:nr, :], X3[0:nr, i, :])
                else:
                    vcopy(cast_eng, xb[0:nr, :], X3[0:nr, i, :])
            for (i, nr), xb in zip(grp, bufs):
                tok0 = i * 128
                for kc in range(8):
                    tr_b(pt3[:, kc, 0:nr], xb[0:nr, kc * 128:(kc + 1) * 128], nr)
                acopy(XT3[:, :, tok0:tok0 + nr], pt3[:, :, 0:nr])

    def flush():
        while pend:
            i, nr, xb = pend.pop(0)
            tok0 = i * 128
            for kc in range(8):
                tr_b(pt3[:, kc, 0:nr], xb[0:nr, kc * 128:(kc + 1) * 128], nr)
            acopy(XT3[:, :, tok0:tok0 + nr], pt3[:, :, 0:nr])

    stat_i = [0]

    def rstd_from(var_ap, out_ap, nr, scale, eps, tmp_ap):
        ts("pool", tmp_ap, var_ap, scale, eps, ALU.mult, ALU.add)
        k = out_ap.shape[1]
        tt("pool", out_ap, tmp_ap, SML[0:nr, 0:k], ALU.pow)

    no_xt = [False]
    final_ln = [False]
    out_done = set()

    def to_xt(i, nr):
        if not no_xt[0]:
            assert len(pend) < 2, "at most two staged tiles (two staging buffers)"
            xb = XBS[xb_i[0] % 2]
            xb_i[0] += 1
            acopy(xb[0:nr, :], X3[0:nr, i, :])
            pend.append((i, nr, xb))

    LNT = SCRF[:, 4096:5120]

    def ln_rows(xin, xout, nr, gb, bb, tmp):
        sl = stat_i[0] % 8
        stat_i[0] += 1
        st6 = STAT[0:nr, sl * 16: sl * 16 + 12]
        mv = STAT[0:nr, sl * 16 + 12: sl * 16 + 14]
        rs = STAT[0:nr, sl * 16 + 14: sl * 16 + 15]
        t1 = STAT[0:nr, sl * 16 + 15: sl * 16 + 16]
        if DEBUG.get('ln_bn', True):
            A("dve", lambda e: e.bn_stats(out=st6[:, 0:6], in_=xin[:, 0:512]), r=[xin[:, 0:512]], w=[st6[:, 0:6]])
            A("dve", lambda e: e.bn_stats(out=st6[:, 6:12], in_=xin[:, 512:1024]), r=[xin[:, 512:1024]], w=[st6[:, 6:12]])
            A("dve", lambda e: e.bn_aggr(out=mv, in_=st6), r=[st6], w=[mv])
            rstd_from(mv[:, 1:2], rs, nr, 1.0, LN_EPS, t1)
        else:
            s1 = st6[:, 0:1]; s2 = st6[:, 1:2]; msq = st6[:, 2:3]
            act(tmp[0:nr, :], xin, AF.Identity, accum=s1)
            act(tmp[0:nr, :], xin, AF.Square, accum=s2)
            ts("pool", mv[:, 0:1], s1, 1.0 / 1024.0, None, ALU.mult)
            tt("pool", msq, mv[:, 0:1], mv[:, 0:1], ALU.mult)
            ts("pool", t1, s2, 1.0 / 1024.0, LN_EPS, ALU.mult, ALU.add)
            tt("pool", t1, t1, msq, ALU.subtract)
            tt("pool", rs, t1, SML[0:nr, 0:1], ALU.pow)
        if not DEBUG.get('ln_pool'):
            stt(tmp[0:nr, :], xin, mv[:, 0:1], gb[0:nr, :], ALU.subtract, ALU.mult)
        else:
            ts("pool", t1, mv[:, 0:1], -1.0, None, ALU.mult)
            ts("pool", tmp[0:nr, :], xin, 1.0, t1, ALU.mult, ALU.add)
            tt("pool", tmp[0:nr, :], tmp[0:nr, :], gb[0:nr, :], ALU.mult)
        stt(xout, tmp[0:nr, :], rs, bb[0:nr, :], ALU.mult, ALU.add)

    def ln_tile(i, nr, slot, tmp=None):
        gb = LNP[:, slot * 2048: slot * 2048 + 1024]
        bb = LNP[:, slot * 2048 + 1024: slot * 2048 + 2048]
        ln_rows(X3[0:nr, i, :], X3[0:nr, i, :], nr, gb, bb, LNT if tmp is None else tmp)
        to_xt(i, nr)
        if no_xt[0] and final_ln[0]:
            if i < 16:
                dma_out(yp[i * 128:(i + 1) * 128, :], X3[:, i, :])
            else:
                dma_out(ys[:, :], X3[0:NS, 16, :])
            out_done.add(i)

    def lnp_load(slot, g_d, b_d, row):
        dma_in(LNP[:, slot * 2048: slot * 2048 + 1024], g_d[row:row + 1, :].partition_broadcast(128))
        dma_in(LNP[:, slot * 2048 + 1024: slot * 2048 + 2048], b_d[row:row + 1, :].partition_broadcast(128))

    def x_acc(i, nr, yps, first):
        xt_ = X3[0:nr, i, :]
        if first:
            stt(xt_, xt_, float(ALPHA), yps, ALU.mult, ALU.add)
        else:
            tt("dve", xt_, xt_, yps, ALU.add)

    def arena3(base, a, b):
        return ARENA[:, base: base + a * b].rearrange("p (a b) -> p a b", a=a)

    def a_base(gi):
        return 0 if gi % 2 == 0 else 14336

    def f_base(gi):
        return 0 if gi % 2 == 0 else 12288

    def load_a_group(j, gi):
        base = a_base(gi)
        wload(arena3(base, 8, 1024), a_win[j, gi].rearrange("p (k m) -> p k m", k=8))
        wload(arena3(base + 8192, 2, 1024), a_wout[j, gi].rearrange("p (h m) -> p h m", h=2))

    def load_f_group(l, gi):
        c0, n = FFN_GROUPS[gi]
        base = f_base(gi)
        src = f_win[l].rearrange("p (k m) -> p k m", k=8)[:, :, c0 * 256:(c0 + n) * 256]
        wload(arena3(base, 8, n * 256), src)
        src2 = f_wout[l].rearrange("p (c m) -> p c m", c=NFC)[:, c0:c0 + n, :]
        wload(arena3(base + 8 * n * 256, n, 1024), src2)

    def load_b_part(j, part):
        if part == "u":
            wload(arena3(0, 8, 1024), b_win[j].rearrange("p (k m) -> p k m", k=8)[:, :, 0:1024])
        elif part == "v0":
            wload(arena3(8192, 8, 512), b_win[j].rearrange("p (k m) -> p k m", k=8)[:, :, 1024:1536])
        elif part == "v1":
            wload(arena3(12288, 8, 512), b_win[j].rearrange("p (k m) -> p k m", k=8)[:, :, 1536:2048])
        else:
            wload(arena3(16384, 8, 1024), b_wout[j].rearrange("p (k m) -> p k m", k=8))

    stages = STAGES[:nstages]

    def prefetch_next_stage(si, early):
        if si >= len(stages):
            return
        kind, l = stages[si][0], int(stages[si][1])
        if kind == "A":
            if early:
                load_a_group(l // 2, 0)
        elif kind == "F":
            if early:
                load_f_group(l, 0)
        else:
            if early:
                load_b_part(l // 2, "u")
                load_b_part(l // 2, "v0")
            else:
                load_b_part(l // 2, "v1")
                load_b_part(l // 2, "o")

    def setup():
        for i in range(16):
            dma_in(X3[:, i, :], xp[i * 128:(i + 1) * 128, :], q=("act" if i % 2 else "sp"))
        dma_in(X3[0:NS, 16, :], xs[:, :])
        dma_in(CM[:], cmask[:, :])
        dma_in(IDF[:], cident[:, :])
        dma_in(LBW[:, 0:32], lbT[:, :])
        memset("dve", SML[:, 0:2], -0.5)
        memset("dve", ONESF[:], 1.0)
        vcopy("dve", IDB[:], IDF[:])
        raw = LBW[:, 0:32].rearrange("p (h l) -> p h l", l=4)
        mx = LBW[:, 32:40]
        A("dve", lambda e: e.tensor_reduce(out=mx, in_=raw, axis=mybir.AxisListType.X, op=ALU.max), r=[raw], w=[mx])
        ex = LBW[:, 40:72].rearrange("p (h l) -> p h l", l=4)
        tt("dve", ex, raw, mx.unsqueeze(2).to_broadcast([128, 8, 4]), ALU.subtract)
        act(ex, ex, AF.Exp)
        sm = LBW[:, 72:80]
        A("dve", lambda e: e.tensor_reduce(out=sm, in_=ex, axis=mybir.AxisListType.X, op=ALU.add), r=[ex], w=[sm])
        rc = LBW[:, 80:88]
        A("dve", lambda e: e.reciprocal(out=rc, in_=sm), r=[sm], w=[rc])
        pr = LBW[:, 88:120].rearrange("p (h l) -> p h l", l=4)
        tt("dve", pr, ex, rc.unsqueeze(2).to_broadcast([128, 8, 4]), ALU.mult)
        cum = LBW[:, 120:128]
        lb = LBW[:, 128:144]
        tt("dve", lb[:, 0:8], pr[:, :, 0], pr[:, :, 0], ALU.subtract)
        tt("dve", cum, pr[:, :, 0], pr[:, :, 1], ALU.add)
        tt("dve", cum, cum, pr[:, :, 2], ALU.add)
        tt("dve", lb[:, 8:16], cum, pr[:, :, 0], ALU.subtract)
        ts("dve", COEF[:, 0:16], lb, 0.5, 0.5, ALU.mult, ALU.add)
        ts("dve", COEF[:, 16:32], lb, -0.5, 0.5, ALU.mult, ALU.add)
        ts("dve", COEF[:, 32:48], lb, 0.5, -0.5, ALU.mult, ALU.add)
        prefetch_next_stage(0, True)
        prefetch_next_stage(0, False)
        xt_convert([(i, 128) for i in range(16)] + [(16, NS)], cast_eng="dve")

    def ffn_stage(si, l):
        lnp_load(0, lnfg, lnfb, l)
        blocks = [(256 * b, 256, [(2 * b, 128), (2 * b + 1, 128)]) for b in range(8)] + [(SEQ, NS, [(16, NS)])]
        SG = [SCRF[:, k * 256:(k + 1) * 256] for k in range(4)]
        AT = [SCRB[:, k * 256:(k + 1) * 256] for k in range(6)]
        cnt = [0]
        ng = len(FFN_GROUPS)
        for gi, (c0, n) in enumerate(FFN_GROUPS):
            base = f_base(gi)
            win = arena3(base, 8, n * 256)
            wout = arena3(base + 8 * n * 256, n, 1024)
            wrelay()
            if gi + 1 < ng:
                load_f_group(l, gi + 1)
            else:
                prefetch_next_stage(si + 1, True)
            steps = [(bi, ci) for bi in range(len(blocks)) for ci in range(n)]
            slots = {}

            def emit_h(bi, ci):
                tok0, ntok, _ = blocks[bi]
                k = cnt[0]
                cnt[0] += 1
                slots[(bi, ci)] = k
                hb = ps[:, (k % 3) * 512:(k % 3) * 512 + 512]
                for gu in range(2):
                    for kc in range(8):
                        mm(hb[:, gu * 256: gu * 256 + ntok], win[:, kc, ci * 256 + gu * 128: ci * 256 + gu * 128 + 128],
                           XT3[:, kc, tok0:tok0 + ntok], kc == 0, kc == 7)
                sg = SG[k % 4][:, 0:ntok]
                act(sg, hb[:, 0:ntok], AF.Silu)
                tt("dve", AT[k % 6][:, 0:ntok], sg, hb[:, 256:256 + ntok], ALU.mult)

            def emit_y(bi, ci):
                tok0, ntok, tiles = blocks[bi]
                k = slots[(bi, ci)]
                last_g = gi == ng - 1
                if ci == n - 1 and last_g:
                    flush()
                for ti, (tile, nr) in enumerate(tiles):
                    yb = ps[0:nr, (3 + 2 * ti) * 512:(3 + 2 * ti) * 512 + 1024]
                    for hh in range(2):
                        mm(yb[:, hh * 512:(hh + 1) * 512], AT[k % 6][:, ti * 128: ti * 128 + nr],
                           wout[:, ci, hh * 512:(hh + 1) * 512], ci == 0, ci == n - 1)
                    if ci == n - 1:
                        x_acc(tile, nr, yb, gi == 0)
                        if last_g:
                            ln_tile(tile, nr, 0)

            SK = 2
            for k in range(len(steps) + SK):
                if k < len(steps):
                    emit_h(*steps[k])
                if k >= SK:
                    emit_y(*steps[k - SK])
        flush()
        prefetch_next_stage(si + 1, False)

    def b_stage(si, l):
        j = l // 2
        lnp_load(1, b_lng, b_lnb, j)
        lnp_load(0, lnmg, lnmb, l)
        Wu = arena3(0, 8, 1024)
        Wv = [arena3(8192, 8, 512), arena3(12288, 8, 512)]
        Wo = arena3(16384, 8, 1024)
        GV = SCRF[:, 0:1024]; VT = SCRF[:, 1024:2048]; VLN = SCRF[:, 2048:3072]
        GUTS = [[SCRF[:, 3072:3584], SCRF[:, 3584:4096]], [SCRF[:, 4096:4608], SCRF[:, 4608:5120]]]
        BIASR = SCRF[0:1, 5120:6144]
        CW = SCRF[0:NS, 6144:6152]; CBc = SCRF[0:NS, 6152:6160]
        GUS = SCRF[0:NS, 0:1024]
        VBS = [SCRB[:, 0:1024], SCRB[:, 3072:4096]]
        PRT = SCRB[:, 1024:2048].rearrange("p (g t) -> p g t", g=8)
        WST = SCRB[:, 2048:3072].rearrange("p (g t) -> p g t", g=8)
        lngb = LNP[:, 2048:3072]; lnbb = LNP[:, 3072:4096]
        vps = ps[:, 0:1024]
        ups = [ps[:, 1024:1536], ps[:, 1536:2048]]
        mps = ps[:, 2048:2560]
        yps = ps[:, 2560:3584]
        wload(WST, b_wsT[j].rearrange("p (g t) -> p g t", g=8))
        wrelay()
        tt("dve", WST, WST, TRILT.unsqueeze(1).to_broadcast([128, 8, 128]), ALU.mult)
        dma_in(BIASR, b_bias[j:j + 1, :])
        dma_in(CW, b_w00[j:j + 1, :].partition_broadcast(NS))
        dma_in(CBc, b_b0[j:j + 1, :].partition_broadcast(NS))

        def front(i):
            nr = 128 if i < 16 else NS
            tok0 = i * 128
            VB = VBS[i % 2]
            GUT = GUTS[i % 2]
            for hh in range(2):
                for kc in range(8):
                    mm(vps[0:nr, hh * 512:(hh + 1) * 512], XT3[:, kc, tok0:tok0 + nr], Wv[hh][:, kc, :], kc == 0, kc == 7)
            act(GV[0:nr, :], vps[0:nr, :], AF.Gelu)
            ln_rows(GV[0:nr, :], VLN[0:nr, :], nr, lngb, lnbb, VT)
            if i == 15:
                dma_out(cvp[j], VLN[:, :])
            if i == 16:
                dma_out(cvs[j], VLN[0:NS, :])
            if i < 16:
                acopy(VB[0:nr, :], VLN[0:nr, :])
                for half in range(2):
                    for g4 in range(4):
                        ch = half * 4 + g4
                        for kc in range(8):
                            mm(ups[half][:, g4 * 128: g4 * 128 + 128], Wu[:, kc, ch * 128:(ch + 1) * 128],
                               XT3[:, kc, tok0:tok0 + 128], kc == 0, kc == 7)
                    act(GUT[half], ups[half], AF.Gelu)
            else:
                u2 = ps[0:NS, 1024:2048]
                for hh in range(2):
                    for kc in range(8):
                        mm(u2[:, hh * 512:(hh + 1) * 512], XT3[:, kc, tok0:tok0 + NS], Wu[:, kc, hh * 512:(hh + 1) * 512], kc == 0, kc == 7)
                act(GUS, u2, AF.Gelu)

        def back(i):
            nr = 128 if i < 16 else NS
            VB = VBS[i % 2]
            GUT = GUTS[i % 2]
            if i < 16:
                for half in range(2):
                    for g4 in range(4):
                        g = half * 4 + g4
                        mm(mps[:, g4 * 128:(g4 + 1) * 128], VB[:, g * 128:(g + 1) * 128], WST[:, g, :], True, False)
                        mm(mps[:, g4 * 128:(g4 + 1) * 128], ONESF[0:1, 0:128], BIASR[0:1, g * 128:(g + 1) * 128], False, True)
                    tt("dve", SCRB[:, 1024 + half * 512: 1024 + (half + 1) * 512], GUT[half], mps, ALU.mult)
            else:
                v3 = VLN[0:NS, :].rearrange("p (g d) -> p g d", g=8)
                m3 = VT[0:NS, :].rearrange("p (g d) -> p g d", g=8)
                tt("dve", m3, v3, CW.unsqueeze(2).to_broadcast([NS, 8, 128]), ALU.mult)
                tt("dve", m3, m3, CBc.unsqueeze(2).to_broadcast([NS, 8, 128]), ALU.add)
                tt("dve", VB[0:NS, :], GUS, VT[0:NS, :], ALU.mult)
                for g in range(8):
                    tr_b(pt3[:, g, 0:NS], VB[0:NS, g * 128:(g + 1) * 128], NS)
                acopy(PRT[:, :, 0:NS], pt3[:, :, 0:NS])
            flush()
            for g in range(8):
                for hh in range(2):
                    mm(yps[0:nr, hh * 512:(hh + 1) * 512], PRT[:, g, 0:nr], Wo[:, g, hh * 512:(hh + 1) * 512], g == 0, g == 7)
            x_acc(i, nr, yps[0:nr, :], True)
            ln_tile(i, nr, 0, VT)

        for t in range(NTILE + 1):
            if t < NTILE:
                front(t)
            if t >= 1:
                back(t - 1)
        flush()
        prefetch_next_stage(si + 1, True)
        prefetch_next_stage(si + 1, False)

    def a_stage(si, l):
        j = l // 2
        lnp_load(0, lnmg, lnmb, l)
        dma_in(NG[:, 0:128], a_ng[j:j + 1, :].partition_broadcast(128))
        dma_in(NG[:, 128:256], a_ng[j:j + 1, :].partition_broadcast(128))
        L1 = LNP[:, 2048:4096]
        AX = ARENA[:, 10240:14336]
        TH = SCRF[:, 0:256]
        SQ = [SCRF[:, 256:512], L1[:, 0:256]]
        FSH = SCRF[:, 512:772]; FS = SCRF[:, 772:1028]; INJ = SCRF[:, 1028:1284]
        KK = [SCRF[:, 1284:1540], L1[:, 256:512]]
        PP = [SCRF[:, 1540:1796], SCRF[:, 1796:2052], L1[:, 512:768]]
        RR = [SCRF[:, 2052:2308], L1[:, 768:1024]]
        TQ = SCRF[:, 2308:2564]
        SGT = [SCRF[:, 2564:2820], L1[:, 1024:1280]]
        GN = [SCRF[:, 2820:3076], SCRF[:, 3076:3332]]; SST = SCRF[:, 3332:3588]; IPC = SCRF[:, 3588:3592]
        SSQ = SCRF[:, 3592:3594]; RST = SCRF[:, 3594:3596]; RTM = SCRF[:, 3596:3598]; JUNK = SCRF[:, 3600:3728]
        OSB = SCRF[:, 3728:3760]
        OSBF = [L1[:, 1280:1408], L1[:, 1408:1536]]
        S0 = [SCRF[:, 5120 + k * 128: 5248 + k * 128] for k in range(8)]
        SN = [SCRF[:, 4096 + k * 128: 4224 + k * 128] for k in range(8)]
        QM = [SCRB[:, 0:512], SCRB[:, 512:1024]]
        QP = SCRB[:, 1024:1280]; KHT = SCRB[:, 1280:1536]
        KTK = [SCRB[:, 1536:1792], SCRB[:, 1792:2048]]
        VB = [SCRB[:, 2048:2304], SCRB[:, 2304:2560], AX[:, 0:256]]
        ATM = [SCRB[:, 2560:2816], SCRB[:, 2816:3072]]
        SB = [SCRB[:, 3072:3328], SCRB[:, 3328:3584]]
        OG = SCRB[:, 3584:3840]; OGT = SCRB[:, 3840:4096]
        KD = SCRB[0:NS, 5120:6144].rearrange("p (b d) -> p b d", b=8)
        pqf = ps[:, 0:512]
        pvg = ps[:, 1024:1536]
        pdS = ps[:, 1536:1792]; pA = ps[:, 1792:2048]
        POH = [ps[:, 2048:2176], ps[:, 512:640]]
        po = ps[:, 2048:2304]
        yps = ps[:, 2560:3584]
        cA = COEF[:, 0:16]; cB = COEF[:, 16:32]; nB = COEF[:, 32:48]
        NCK = 128 // CH

        memset("dve", FSH, 0.0)
        memset("dve", FS, 0.0)
        memset("dve", INJ, 0.0)
        memset("dve", INJ[:, CH - 1:256:CH], 1.0)
        memset("dve", QM[0], 0.0)
        memset("dve", QM[1], 0.0)
        pstep = list(SCRF[:].ap[0])
        bstep = list(SCRB[:].ap[0])
        scrf_t = SCRF[:].tensor
        scrb_t = SCRB[:].tensor
        lstep = list(LNP[:].ap[0])
        lnp_t = LNP[:].tensor
        fsh_rev = bass.AP(scrf_t, 512 + 256, [pstep, [-1, 256]])
        inj_rev = bass.AP(scrf_t, 1028 + 255, [pstep, [-1, 256]])
        rr_rev = [bass.AP(scrf_t, 2052 + 255, [pstep, [-1, 256]]), bass.AP(lnp_t, 2048 + 768 + 255, [lstep, [-1, 256]])]

        for gi in range(DEBUG.get('ngroups', 4)):
            base = a_base(gi)
            win = ARENA[:, base: base + 8192].rearrange("p (k s m) -> p k s m", k=8, s=4)
            wout = arena3(base + 8192, 2, 1024)
            if gi + 1 < 4:
                load_a_group(j, gi + 1)
            else:
                prefetch_next_stage(si + 1, True)
            hcol = [j * 8 + 2 * gi, j * 8 + 2 * gi + 1]
            memset("dve", SST, 0.0)
            memset("dve", SB[0], 0.0)
            cur = [0]

            def p0(i):
                tok0 = i * 128
                for sec in range(2):
                    for hh in range(2):
                        for kc in range(8):
                            mm(pqf[:, sec * 256 + hh * 128: sec * 256 + hh * 128 + 128],
                               win[:, kc, sec, hh * 128:(hh + 1) * 128], XT3[:, kc, tok0:tok0 + 128], kc == 0, kc == 7)
                act(TH, pqf[:, 256:512], AF.Tanh, scale=0.5)
                act(SQ[i % 2], pqf[:, 0:256], AF.Silu)

            def p1(i):
                tok0 = i * 128
                for sec in range(2):
                    for kc in range(8):
                        mm(pvg[:, sec * 256:(sec + 1) * 256], XT3[:, kc, tok0:tok0 + 128], win[:, kc, 2 + sec, :], kc == 0, kc == 7)
                acopy(VB[i % 3], pvg[:, 0:256])
                act(SGT[i % 2], pvg[:, 256:512], AF.Silu)
                for hh in range(2):
                    c = hcol[hh]
                    ts("dve", FSH[:, hh * 128:(hh + 1) * 128], TH[:, hh * 128:(hh + 1) * 128], cB[:, c:c + 1], cA[:, c:c + 1], ALU.mult, ALU.add)
                    ts("pool", KK[i % 2][:, hh * 128:(hh + 1) * 128], TH[:, hh * 128:(hh + 1) * 128], nB[:, c:c + 1], cB[:, c:c + 1], ALU.mult, ALU.add)
                vcopy("dve", FS[:, 0:256:CH], FSH[:, 0:256:CH])
                memset("dve", FSH[:, 0:256:CH], 0.0)
                pp = PP[i % 3]
                A("dve", lambda e: e.tensor_tensor_scan(out=pp, data0=FSH[:, 0:256], data1=FS, initial=0.0, op0=ALU.mult, op1=ALU.add),
                  r=[FSH[:, 0:256], FS], w=[pp])
                rrv = rr_rev[i % 2]
                A("dve", lambda e: e.tensor_tensor_scan(out=rrv, data0=fsh_rev, data1=inj_rev, initial=0.0, op0=ALU.mult, op1=ALU.add),
                  r=[FSH[:, 1:257], INJ], w=[RR[i % 2]])

            def p2(i):
                pb = i % 2
                pp = PP[i % 3]
                qmd = bass.AP(scrb_t, pb * 512, [bstep, [256, 2], [128 + CH, NCK], [1, CH]])
                stt(TQ, SQ[pb], float(QSCALE), pp, ALU.mult, ALU.mult)
                A("dve", lambda e: e.reciprocal(out=IPC, in_=pp[:, CH - 1:256:CH]), r=[pp], w=[IPC])
                qeng = DEBUG.get('q_eng', 'dve')
                A(qeng, lambda e: e.tensor_copy(out=qmd, in_=TQ.rearrange("p (h c i) -> p h c i", h=2, c=NCK)), r=[TQ], w=[QM[pb]])
                tt(qeng, QP.rearrange("p (c i) -> p c i", i=CH), TQ.rearrange("p (c i) -> p c i", i=CH),
                   IPC.unsqueeze(2).to_broadcast([128, 2 * NCK, CH]), ALU.mult)
                tt("pool", KHT, KK[pb], RR[pb], ALU.mult)
                tt("pool", GN[pb], SGT[pb], NG[:, :], ALU.mult)

            def p3(i):
                pb = i % 2
                for hh in range(2):
                    tr_b(pt[:, hh * 128:(hh + 1) * 128], KHT[:, hh * 128:(hh + 1) * 128], 128)
                vcopy('dve', KTK[pb], pt[:, 0:256])
                for hh in range(2):
                    mm(pA[:, hh * 128:(hh + 1) * 128], KHT[:, hh * 128:(hh + 1) * 128], QP[:, hh * 128:(hh + 1) * 128], True, True)
                tt("dve", ATM[pb].rearrange("p (h t) -> p h t", h=2), pA.rearrange("p (h t) -> p h t", h=2),
                   MASKA.unsqueeze(1).to_broadcast([128, 2, 128]), ALU.mult)

            def fin1a(nr, o_src):
                for hh in range(2):
                    act(JUNK[0:nr, :], o_src[hh][0:nr, :], AF.Square, accum=SSQ[0:nr, hh:hh + 1])
                rstd_from(SSQ[0:nr, :], RST[0:nr, :], nr, 1.0 / 128.0, RMS_EPS, RTM[0:nr, :])

            def fin1b(nr, o_src, gn):
                for hh in range(2):
                    stt(OG[0:nr, hh * 128:(hh + 1) * 128], o_src[hh][0:nr, :], RST[0:nr, hh:hh + 1],
                        gn[0:nr, hh * 128:(hh + 1) * 128], ALU.mult, ALU.mult)

            def fin2(nr, tile, first, last):
                fin2a(nr)
                fin2b(nr, tile, first, last)

            def fin2b(nr, tile, first, last):
                x_acc(tile, nr, yps[0:nr, :], first)
                if last:
                    ln_tile(tile, nr, 0)

            def fin2a(nr):
                for hh in range(2):
                    tr_b(pt[:, 256 + hh * 128: 256 + hh * 128 + nr], OG[0:nr, hh * 128:(hh + 1) * 128], nr)
                if nr == 128:
                    vcopy("dve", OGT, pt[:, 256:512])
                else:
                    vcopy("dve", OGT.rearrange("p (h t) -> p h t", h=2)[:, :, 0:nr], pt[:, 256:512].rearrange("p (h t) -> p h t", h=2)[:, :, 0:nr])
                for hh in range(2):
                    for half in range(2):
                        mm(yps[0:nr, half * 512:(half + 1) * 512], OGT[:, hh * 128: hh * 128 + nr], wout[:, hh, half * 512:(half + 1) * 512], hh == 0, hh == 1)

            def chain(i, jc):
                pb = i % 2
                for hh in range(2):
                    c = cur[0]
                    mm(POH[hh], QM[pb][:, hh * 256 + jc * 128: hh * 256 + jc * 128 + 128],
                       SB[c][:, hh * 128:(hh + 1) * 128], False, jc == NCK - 1)
                    mm(pdS[:, hh * 128:(hh + 1) * 128], KTK[pb][jc * CH:(jc + 1) * CH, hh * 128:(hh + 1) * 128],
                       VB[i % 3][jc * CH:(jc + 1) * CH, hh * 128:(hh + 1) * 128], True, True)
                    pc = PP[i % 3][:, hh * 128 + jc * CH + CH - 1: hh * 128 + jc * CH + CH]
                    stt(SST[:, hh * 128:(hh + 1) * 128], SST[:, hh * 128:(hh + 1) * 128], pc, pdS[:, hh * 128:(hh + 1) * 128], ALU.mult, ALU.add)
                    acopy(SB[1 - c][:, hh * 128:(hh + 1) * 128], SST[:, hh * 128:(hh + 1) * 128])
                cur[0] = 1 - cur[0]

            NTL = DEBUG.get('ntiles', 16)
            p0(0); p1(0)
            if NTL > 1:
                p0(1); p1(1)
            p2(0); p3(0)
            for t in range(NTL):
                pb = t % 2
                for hh in range(2):
                    mm(POH[hh], ATM[pb][:, hh * 128:(hh + 1) * 128], VB[t % 3][:, hh * 128:(hh + 1) * 128], True, False)
                chain(t, 0)
                if t >= 1:
                    fin2a(128)
                if t + 1 < NTL:
                    p2(t + 1)
                if t >= 1:
                    fin2b(128, t - 1, gi == 0, gi == 3)
                if t + 2 < NTL:
                    p0(t + 2)
                flush()
                for jc in range(1, NCK):
                    chain(t, jc)
                for hh in range(2):
                    acopy(OSBF[hh], POH[hh])
                fin1a(128, OSBF)
                if t + 1 < NTL:
                    p3(t + 1)
                if t + 2 < NTL:
                    p1(t + 2)
                fin1b(128, OSBF, GN[pb])
            fin2(128, NTL - 1, gi == 0, gi == 3)
            flush()
            for hh in range(2):
                dma_out(sp_o[j, 2 * gi + hh], SST[:, hh * 128:(hh + 1) * 128])

            if DEBUG.get('nosample'):
                continue
            tok0 = SEQ
            for sec in range(2):
                for hh in range(2):
                    for kc in range(8):
                        mm(pqf[:, sec * 32 + hh * 16: sec * 32 + hh * 16 + 16], win[:, kc, sec, hh * 128:(hh + 1) * 128],
                           XT3[:, kc, tok0:tok0 + NS], kc == 0, kc == 7)
            for sec in range(2):
                for kc in range(8):
                    mm(pvg[0:NS, sec * 256:(sec + 1) * 256], XT3[:, kc, tok0:tok0 + NS], win[:, kc, 2 + sec, :], kc == 0, kc == 7)
            act(TH[:, 0:32], pqf[:, 32:64], AF.Tanh, scale=0.5)
            act(SQ[0][:, 0:32], pqf[:, 0:32], AF.Silu)
            acopy(VB[0][0:NS, :], pvg[0:NS, 0:256])
            act(SGT[0][0:NS, :], pvg[0:NS, 256:512], AF.Silu)
            FSS = SCRF[:, 3760:3792]
            KKS = SCRF[:, 3792:3824]
            QSS = SCRF[:, 3824:3856]
            for hh in range(2):
                c = hcol[hh]
                ts("dve", FSS[:, hh * 16:(hh + 1) * 16], TH[:, hh * 16:(hh + 1) * 16], cB[:, c:c + 1], cA[:, c:c + 1], ALU.mult, ALU.add)
                ts("pool", KKS[:, hh * 16:(hh + 1) * 16], TH[:, hh * 16:(hh + 1) * 16], nB[:, c:c + 1], cB[:, c:c + 1], ALU.mult, ALU.add)
            ts("dve", QSS, SQ[0][:, 0:32], float(QSCALE), None, ALU.mult)
            tt("pool", GN[0][0:NS, :], SGT[0][0:NS, :], NG[0:NS, :], ALU.mult)
            for hh in range(2):
                tr_f(pA[0:NS, hh * 128:(hh + 1) * 128], KKS[:, hh * 16:(hh + 1) * 16], 128)
            DI = L1[:, 1536:1792]
            QD = SCRB[:, 4608:5120].rearrange("p (h b m) -> p h b m", h=2, b=NS)
            SNB = [SCRB[:, 4096 + k * 128: 4224 + k * 128] for k in range(4)]
            memset("dve", DI, 0.0)
            memset("dve", DI[:, 0:256:17], 1.0)
            for hh in range(2):
                tt("dve", QD[:, hh], QSS[:, hh * 16:(hh + 1) * 16].unsqueeze(2).to_broadcast([128, NS, NS]),
                   DI.rearrange("p (b m) -> p b m", b=NS), ALU.mult)
            po2 = ps[:, 2304:2560]
            order = [(hh, bh, b8) for hh in range(2) for bh in range(2) for b8 in range(8)]
            NSL = 8

            def ld(kk):
                hh_, bh_, b8_ = order[kk]
                dma_in(S0[kk % 8], st[j, bh_ * 8 + b8_, 2 * gi + hh_])

            def kv(kk):
                hh_, bh_, b8_ = order[kk]
                if b8_ == 0:
                    tt("dve", KD, pA[0:NS, hh_ * 128:(hh_ + 1) * 128].unsqueeze(1).to_broadcast([NS, 8, 128]),
                       IDF[0:NS, bh_ * 8:(bh_ + 1) * 8].unsqueeze(2).to_broadcast([NS, 8, 128]), ALU.mult)
                dsl = pdS[:, (kk % 2) * 128:(kk % 2) * 128 + 128]
                mm(dsl, KD[:, b8_, :], VB[0][0:NS, hh_ * 128:(hh_ + 1) * 128], True, True)
                b_ = bh_ * 8 + b8_
                stt(SN[kk % NSL], S0[kk % 8], FSS[:, hh_ * 16 + b_: hh_ * 16 + b_ + 1], dsl, ALU.mult, ALU.add)
                dma_out(ss_o[j, b_, 2 * gi + hh_], SN[kk % NSL], q=("act" if kk % 2 else "sp"))
                acopy(SNB[kk % 4], SN[kk % NSL])

            def omm(kk):
                hh_, bh_, b8_ = order[kk]
                b_ = bh_ * 8 + b8_
                mm(po2[0:NS, hh_ * 128:(hh_ + 1) * 128], QD[:, hh_, b_, :], SNB[kk % 4], b_ == 0, b_ == NS - 1)

            for kk in range(7):
                ld(kk)
            LAG = 3
            for kk in range(len(order) + LAG):
                if kk < len(order):
                    if kk + 7 < len(order):
                        ld(kk + 7)
                    kv(kk)
                if kk >= LAG:
                    omm(kk - LAG)
            osrc = [po2[:, 0:128], po2[:, 128:256]]
            fin1a(NS, osrc)
            fin1b(NS, osrc, GN[0])
            fin2(NS, 16, gi == 0, gi == 3)
            flush()
        prefetch_next_stage(si + 1, False)

    setup()
    for si, name in enumerate(stages):
        l = int(name[1])
        no_xt[0] = (si == len(stages) - 1)
        final_ln[0] = no_xt[0] and name[0] == "F"
        if name[0] == "A":
            a_stage(si, l)
        elif name[0] == "F":
            ffn_stage(si, l)
        else:
            b_stage(si, l)
    flush()
    for i in range(16):
        if i not in out_done:
            dma_out(yp[i * 128:(i + 1) * 128, :], X3[:, i, :])
    if 16 not in out_done:
        dma_out(ys[:, :], X3[0:NS, 16, :])
    run_sched(nc, S, [], es.enter_context)
    es.close()
    return nc


def _shared_layouts(inp):
    f = lambda a: np.ascontiguousarray(np.asarray(a, dtype=np.float32))
    a_w_in = f(inp["a_w_in"]); a_w_out = f(inp["a_w_out"])
    b_w_in = f(inp["b_w_in"]); b_w_out = f(inp["b_w_out"]); b_w_s = f(inp["b_w_s"]); b_bias_s = f(inp["b_bias_s"])
    ffn_w_in = f(inp["ffn_w_in"]); ffn_w_out = f(inp["ffn_w_out"])
    sh = {}
    sh["lnmg"] = f(inp["ln_mix_g"]); sh["lnmb"] = f(inp["ln_mix_b"])
    sh["lnfg"] = f(inp["ln_ffn_g"]); sh["lnfb"] = f(inp["ln_ffn_b"])
    sh["lbT"] = f(f(inp["a_lb_raw"]).reshape(4, 8, 128).transpose(2, 1, 0).reshape(128, 32))
    sh["a_win"] = f(a_w_in.reshape(2, 8, 128, 4, 4, 2, 128).transpose(0, 4, 2, 1, 3, 5, 6).reshape(2, 4, 128, 8192))
    sh["a_wout"] = f(a_w_out.reshape(2, 4, 2, 128, 1024).transpose(0, 1, 3, 2, 4).reshape(2, 4, 128, 2048))
    sh["a_ng"] = f(inp["a_norm_g"])
    sh["b_win"] = f(b_w_in.reshape(2, 8, 128, 2048).transpose(0, 2, 1, 3).reshape(2, 128, 8 * 2048))
    sh["b_wout"] = f(b_w_out.reshape(2, 8, 128, 1024).transpose(0, 2, 1, 3).reshape(2, 128, 8 * 1024))
    sh["b_lng"] = f(inp["b_ln_g"]); sh["b_lnb"] = f(inp["b_ln_b"])
    sh["b_wsT"] = f(b_w_s.transpose(0, 3, 1, 2).reshape(2, 128, 1024))
    sh["b_bias"] = f(b_bias_s.reshape(2, 1024))
    sh["b_w00"] = f(b_w_s[:, :, 0, 0]); sh["b_b0"] = f(b_bias_s[:, :, 0])
    sh["f_win"] = f(ffn_w_in.reshape(4, 8, 128, 2, NFC, 128).transpose(0, 2, 1, 4, 3, 5).reshape(4, 128, 8 * 2 * DFF))
    sh["f_wout"] = f(ffn_w_out.reshape(4, NFC, 128, 1024).transpose(0, 2, 1, 3).reshape(4, 128, NFC * 1024))
    idx = np.arange(128)
    maska = ((idx[:, None] // CH) == (idx[None, :] // CH)) & (idx[:, None] <= idx[None, :])
    tril = idx[:, None] <= idx[None, :]
    sh["cmask"] = f(np.concatenate([maska, tril], axis=1))
    sh["cident"] = f(np.eye(128))
    return sh


def _core_inputs(inp, sh, c):
    f = lambda a: np.ascontiguousarray(np.asarray(a, dtype=np.float32))
    m = dict(sh)
    m["xp"] = f(np.asarray(inp["x_prompt"])[c])
    m["xs"] = f(np.asarray(inp["x_sample"])[c * NS:(c + 1) * NS, 0, :])
    m["st"] = f(np.asarray(inp["state_hgrn"])[:, c * NS:(c + 1) * NS])
    return m


def run_cores(inp, cores, nstages=8):
    nc = build_program(nstages)
    sh = _shared_layouts(inp)
    in_maps = [_core_inputs(inp, sh, c) for c in cores]
    res = run_bass_kernel_spmd(nc, in_maps, core_ids=list(range(len(cores))))
    return res.results


def kernel(**inputs):
    res = run_cores(inputs, list(range(NCORES)), 8)
    y_prompt = np.zeros((NCORES, SEQ, D), np.float32)
    y_sample = np.zeros((NCORES * NS, 1, D), np.float32)
    sp = np.zeros((2, NCORES, 8, 128, 128), np.float32)
    ss = np.zeros((2, NCORES * NS, 8, 128, 128), np.float32)
    cvp = np.zeros((2, NCORES, 128, D), np.float32)
    cvs = np.zeros((2, NCORES * NS, 1, D), np.float32)
    for c, r in enumerate(res):
        y_prompt[c] = r["yp"]
        y_sample[c * NS:(c + 1) * NS, 0] = r["ys"]
        sp[:, c] = r["sp"]
        ss[:, c * NS:(c + 1) * NS] = r["ss"]
        cvp[:, c] = r["cvp"]
        cvs[:, c * NS:(c + 1) * NS, 0] = r["cvs"]
    return (y_prompt, y_sample, sp, ss, cvp, cvs)
```

```python
import numpy as np
import concourse.bass as bass
import concourse.mybir as mybir
from concourse.bass_utils import run_bass_kernel_spmd

F32 = mybir.dt.float32
BF16 = mybir.dt.bfloat16
AF = mybir.ActivationFunctionType
ALU = mybir.AluOpType

NCORES = 8
D = 1024
SEQ = 2048
NS = 16
NT = SEQ + NS
NTILE = 17
DEPTH = 4
DFF = 2816
NFC = DFF // 128
ALPHA = (2 * DEPTH) ** 0.25
LN_EPS = 1e-5
RMS_EPS = 1e-6
QSCALE = 128 ** -0.5
SEM_ROT = 1 << 30


class Op:
    __slots__ = ("eng", "fn", "deps", "signal", "sem", "val", "is_dma", "idx", "clear")


def _ap_range(ap):
    t = ap.tensor
    dims = list(ap.ap)
    rowlen = dims[0][0]
    off = int(ap.offset)
    if rowlen <= 0:
        rowlen = 1 << 40
    plo = off // rowlen
    clo = off % rowlen
    phi = plo + dims[0][1]
    lo = clo
    hi = clo
    for st, cnt in dims[1:]:
        if st >= 0:
            hi += st * (cnt - 1)
        else:
            lo += st * (cnt - 1)
    return (t.name, plo, phi, lo, hi + 1)


class Sched:
    ENG = ("pe", "act", "dve", "pool", "sp")

    def __init__(self, nc):
        self.nc = nc
        self.ops = {e: [] for e in self.ENG}
        self.recs = {}
        self.nsig = {e: 0 for e in self.ENG}
        self.ndma = 0
        self.dma_last = {}
        self.NDMASEM = 16
        self.psum_bank = {"ps": 512, "pt": 1024}
        self.NSW = 8
        self.nsw = 0
        self.sw_pending = {}

    def add(self, eng, fn, reads=(), writes=(), dma=False, extra=(), sw=False, clear=()):
        op = Op()
        op.clear = list(clear)
        op.eng = eng
        op.fn = fn
        op.signal = False
        op.sem = None
        op.val = None
        op.is_dma = dma
        op.idx = len(self.ops[eng])
        deps = set(extra)
        for ap, isw in [(a, False) for a in reads] + [(a, True) for a in writes]:
            name, plo, phi, lo, hi = _ap_range(ap)
            bank = self.psum_bank.get(name)
            if bank is not None and isw and eng == "pe":
                lo = (lo // bank) * bank
                hi = -(-hi // bank) * bank
                plo, phi = 0, 128
            lst = self.recs.setdefault(name, [])
            keep = []
            for r in lst:
                ov = not (r[1] <= plo or phi <= r[0] or r[3] <= lo or hi <= r[2])
                if ov and (isw or r[5]):
                    if r[4] is not op:
                        deps.add(r[4])
                cov = plo <= r[0] and r[1] <= phi and lo <= r[2] and r[3] <= hi
                if cov and (isw or (not r[5] and r[4].eng == eng and not r[4].is_dma and not dma)):
                    continue
                keep.append(r)
            keep.append([plo, phi, lo, hi, op, isw])
            self.recs[name] = keep
        if dma and sw:
            op.sem = ("sw", self.nsw)
            self.nsw += 1
            op.val = 16
            op.signal = True
        elif dma:
            slot = self.ndma % self.NDMASEM
            self.ndma += 1
            prev = self.dma_last.get(slot)
            if prev is not None:
                deps.add(prev)
            self.dma_last[slot] = op
            op.sem = ("dma", slot)
            op.val = 16 * ((self.ndma - 1) // self.NDMASEM + 1)
            op.signal = True
        op.deps = [d for d in deps if not (d.eng == "pe" and eng == "pe" and not d.is_dma and not dma)]
        for d in op.deps:
            d.signal = True
        for key in op.clear:
            self.sw_pending.pop(key[1], None)
        self.ops[eng].append(op)
        return op


def run_sched(nc, S, final_ops, ctx_enter):
    last_ops = []
    for e in S.ENG:
        comp = [op for op in S.ops[e] if not op.is_dma]
        if comp:
            comp[-1].signal = True
            last_ops.append(comp[-1])
    sem_objs = {}
    for e in S.ENG:
        n = 0
        for op in S.ops[e]:
            if op.is_dma or not op.signal:
                continue
            key = (e, n // SEM_ROT)
            op.sem = key
            op.val = n % SEM_ROT + 1
            n += 1
            sem_objs.setdefault(key, None)
    for slot in range(min(S.ndma, S.NDMASEM)):
        sem_objs[("dma", slot)] = None
    for slot in range(S.nsw):
        sem_objs[("sw", slot)] = None
    for key in list(sem_objs):
        sem_objs[key] = ctx_enter(nc.semaphore("s_%s_%d" % key))
    def emit_engine(ename, eng):
        waited = {}
        for op in S.ops[ename]:
            best = {}
            for d in op.deps:
                if d.val is None:
                    continue
                if best.get(d.sem, 0) < d.val:
                    best[d.sem] = d.val
            for key, v in best.items():
                if waited.get(key, 0) >= v:
                    continue
                eng.wait_ge(sem_objs[key], v)
                waited[key] = v
                if DEBUG.get('trace'):
                    print('   ', ename, 'WAIT', key, v)
            for key in op.clear:
                eng.sem_clear(sem_objs[key])
                waited.pop(key, None)
            ins = op.fn(eng)
            if DEBUG.get('trace'):
                print(ename, op.idx, 'sig' if op.signal else '', op.sem, op.val, 'dma' if op.is_dma else '')
            if op.is_dma:
                ins.then_inc(sem_objs[op.sem], 16)
            elif op.signal:
                ins.then_inc(sem_objs[op.sem], 1)
        if ename == "sp":
            for slot, op in S.dma_last.items():
                eng.wait_ge(sem_objs[op.sem], op.val)
            for slot in range(S.nsw):
                eng.wait_ge(sem_objs[("sw", slot)], 16)
            for op in last_ops:
                eng.wait_ge(sem_objs[op.sem], op.val)

    block = ctx_enter(nc.Block())

    @block.tensor
    def _(eng):
        emit_engine("pe", eng)

    @block.scalar
    def _(eng):
        emit_engine("act", eng)

    @block.vector
    def _(eng):
        emit_engine("dve", eng)

    @block.gpsimd
    def _(eng):
        emit_engine("pool", eng)

    @block.sync
    def _(eng):
        emit_engine("sp", eng)


import contextlib

STAGES = ["A0", "F0", "B1", "F1", "A2", "F2", "B3", "F3"]
FFN_GROUPS = [(0, 2), (2, 4), (6, 4), (10, 4), (14, 4), (18, 4)]
CH = 64
DEBUG = {}


def build_program(nstages=8):
    nc = bass.Bass("TRN2", target_bir_lowering=False)
    es = contextlib.ExitStack()

    def din(name, shape):
        return nc.dram_tensor(name, list(shape), F32, kind="ExternalInput").ap()

    def dout(name, shape):
        return nc.dram_tensor(name, list(shape), F32, kind="ExternalOutput").ap()

    xp = din("xp", [SEQ, D]); xs = din("xs", [NS, D]); st = din("st", [2, NS, 8, 128, 128])
    lnmg = din("lnmg", [4, D]); lnmb = din("lnmb", [4, D]); lnfg = din("lnfg", [4, D]); lnfb = din("lnfb", [4, D])
    lbT = din("lbT", [128, 32])
    a_win = din("a_win", [2, 4, 128, 8192]); a_wout = din("a_wout", [2, 4, 128, 2048]); a_ng = din("a_ng", [2, 128])
    b_win = din("b_win", [2, 128, 8 * 2048]); b_wout = din("b_wout", [2, 128, 8 * 1024])
    b_lng = din("b_lng", [2, D]); b_lnb = din("b_lnb", [2, D]); b_wsT = din("b_wsT", [2, 128, 1024])
    b_bias = din("b_bias", [2, D]); b_w00 = din("b_w00", [2, 8]); b_b0 = din("b_b0", [2, 8])
    f_win = din("f_win", [4, 128, 8 * 2 * DFF]); f_wout = din("f_wout", [4, 128, NFC * 1024])
    cmask = din("cmask", [128, 256]); cident = din("cident", [128, 128])

    yp = dout("yp", [SEQ, D]); ys = dout("ys", [NS, D]); sp_o = dout("sp", [2, 8, 128, 128])
    ss_o = dout("ss", [2, NS, 8, 128, 128]); cvp = dout("cvp", [2, 128, D]); cvs = dout("cvs", [2, NS, D])

    def T(name, shape, dt):
        return es.enter_context(nc.sbuf_tensor(name, list(shape), dt))

    X = T("X", [128, NTILE * D], F32)
    XT = T("XT", [128, 8 * NT], BF16)
    ARENA = T("ARENA", [128, 24576], BF16)
    LNP = T("LNP", [128, 4096], F32)
    CM = T("CM", [128, 256], F32)
    IDF = T("IDF", [128, 128], F32)
    IDB = T("IDB", [128, 128], BF16)
    ONESF = T("ONESF", [128, 128], F32)
    NG = T("NG", [128, 256], F32)
    COEF = T("COEF", [128, 64], F32)
    LBW = T("LBW", [128, 160], F32)
    STAT = T("STAT", [128, 128], F32)
    SML = T("SML", [128, 64], F32)
    SCRF = T("SCRF", [128, 6400], F32)
    SCRB = T("SCRB", [128, 6144], BF16)
    ps = es.enter_context(nc.psum_tensor("ps", [128, 7 * 512], F32))
    pt = es.enter_context(nc.psum_tensor("pt", [128, 1024], BF16))

    X3 = X[:].rearrange("p (i d) -> p i d", d=D)
    XT3 = XT[:].rearrange("p (k t) -> p k t", k=8)
    pt3 = pt[:].rearrange("p (k t) -> p k t", k=8)
    NEGHALF = SML[:, 0:2]
    MASKA = CM[:, 0:128]
    TRILT = CM[:, 128:256]

    S = Sched(nc)

    def A(eng, fn, r=(), w=(), dma=False):
        return S.add(eng, fn, reads=r, writes=w, dma=dma)

    def mm(out, lhsT, rhs, start, stop):
        A("pe", lambda e: e.matmul(out, lhsT=lhsT, rhs=rhs, start=start, stop=stop), r=[lhsT, rhs], w=[out])

    def tr_b(out, in_, nr):
        idn = IDB[0:nr, 0:nr]
        A("pe", lambda e: e.transpose(out=out, in_=in_, identity=idn), r=[in_, idn], w=[out])

    def tr_f(out, in_, nr):
        idn = IDF[0:nr, 0:nr]
        A("pe", lambda e: e.transpose(out=out, in_=in_, identity=idn), r=[in_, idn], w=[out])

    def act(out, in_, func, scale=1.0, accum=None):
        if accum is None:
            A("act", lambda e: e.activation(out=out, in_=in_, func=func, scale=scale), r=[in_], w=[out])
        else:
            A("act", lambda e: e.activation(out=out, in_=in_, func=func, scale=scale, accum_out=accum), r=[in_], w=[out, accum])

    def acopy(out, in_):
        A("act", lambda e: e.copy(out=out, in_=in_), r=[in_], w=[out])

    def tt(eng, out, in0, in1, op):
        A(eng, lambda e: e.tensor_tensor(out=out, in0=in0, in1=in1, op=op), r=[in0, in1], w=[out])

    def ts(eng, out, in0, s1, s2, op0, op1=None):
        rs = [in0] + [s for s in (s1, s2) if not isinstance(s, (int, float)) and s is not None]
        if op1 is None:
            A(eng, lambda e: e.tensor_scalar(out=out, in0=in0, scalar1=s1, scalar2=None, op0=op0), r=rs, w=[out])
        else:
            A(eng, lambda e: e.tensor_scalar(out=out, in0=in0, scalar1=s1, scalar2=s2, op0=op0, op1=op1), r=rs, w=[out])

    def stt(out, in0, scalar, in1, op0, op1):
        rs = [in0, in1] + ([] if isinstance(scalar, (int, float)) else [scalar])
        A("dve", lambda e: e.scalar_tensor_tensor(out=out, in0=in0, scalar=scalar, in1=in1, op0=op0, op1=op1), r=rs, w=[out])

    def vcopy(eng, out, in_):
        A(eng, lambda e: e.tensor_copy(out=out, in_=in_), r=[in_], w=[out])

    def memset(eng, ap, val):
        A(eng, lambda e: e.memset(ap, val), w=[ap])

    def dma_in(out, in_, q="sp"):
        return A(q, lambda e: e.dma_start(out=out, in_=in_), w=[out], dma=True)

    def dma_out(out, in_, q="sp"):
        return A(q, lambda e: e.dma_start(out=out, in_=in_), r=[in_], dma=True)

    def wload(dst, src):
        return S.add("pool", lambda e: e.dma_start(out=dst, in_=src, max_dma_last_dim=4096), writes=[dst], dma=True, sw=True)

    def wrelay():
        pass

    pend = []
    XBS = [SCRB[:, 4096:5120], SCRB[:, 5120:6144]]
    xb_i = [0]

    def xt_convert(items, cast_eng="act"):
        for g0 in range(0, len(items), 2):
            grp = items[g0:g0 + 2]
            bufs = []
            for (i, nr) in grp:
                xb = XBS[xb_i[0] % 2]
                xb_i[0] += 1
                bufs.append(xb)
                if cast_eng == "act":
                    acopy(xb[0:nr, :], X3[0:nr, i, :])
                else:
                    vcopy(cast_eng, xb[0:nr, :], X3[0:nr, i, :])
            for (i, nr), xb in zip(grp, bufs):
                tok0 = i * 128
                for kc in range(8):
                    tr_b(pt3[:, kc, 0:nr], xb[0:nr, kc * 128:(kc + 1) * 128], nr)
                acopy(XT3[:, :, tok0:tok0 + nr], pt3[:, :, 0:nr])

    def flush():
        while pend:
            i, nr, xb = pend.pop(0)
            tok0 = i * 128
            for kc in range(8):
                tr_b(pt3[:, kc, 0:nr], xb[0:nr, kc * 128:(kc + 1) * 128], nr)
            acopy(XT3[:, :, tok0:tok0 + nr], pt3[:, :, 0:nr])

    stat_i = [0]

    def rstd_from(var_ap, out_ap, nr, scale, eps, tmp_ap):
        ts("pool", tmp_ap, var_ap, scale, eps, ALU.mult, ALU.add)
        k = out_ap.shape[1]
        tt("pool", out_ap, tmp_ap, SML[0:nr, 0:k], ALU.pow)

    no_xt = [False]
    final_ln = [False]
    out_done = set()

    def to_xt(i, nr):
        if not no_xt[0]:
            assert len(pend) < 2, "at most two staged tiles (two staging buffers)"
            xb = XBS[xb_i[0] % 2]
            xb_i[0] += 1
            acopy(xb[0:nr, :], X3[0:nr, i, :])
            pend.append((i, nr, xb))

    LNT = SCRF[:, 4096:5120]

    def ln_rows(xin, xout, nr, gb, bb, tmp):
        sl = stat_i[0] % 8
        stat_i[0] += 1
        st6 = STAT[0:nr, sl * 16: sl * 16 + 12]
        mv = STAT[0:nr, sl * 16 + 12: sl * 16 + 14]
        rs = STAT[0:nr, sl * 16 + 14: sl * 16 + 15]
        t1 = STAT[0:nr, sl * 16 + 15: sl * 16 + 16]
        if DEBUG.get('ln_bn', True):
            A("dve", lambda e: e.bn_stats(out=st6[:, 0:6], in_=xin[:, 0:512]), r=[xin[:, 0:512]], w=[st6[:, 0:6]])
            A("dve", lambda e: e.bn_stats(out=st6[:, 6:12], in_=xin[:, 512:1024]), r=[xin[:, 512:1024]], w=[st6[:, 6:12]])
            A("dve", lambda e: e.bn_aggr(out=mv, in_=st6), r=[st6], w=[mv])
            rstd_from(mv[:, 1:2], rs, nr, 1.0, LN_EPS, t1)
        else:
            s1 = st6[:, 0:1]; s2 = st6[:, 1:2]; msq = st6[:, 2:3]
            act(tmp[0:nr, :], xin, AF.Identity, accum=s1)
            act(tmp[0:nr, :], xin, AF.Square, accum=s2)
            ts("pool", mv[:, 0:1], s1, 1.0 / 1024.0, None, ALU.mult)
            tt("pool", msq, mv[:, 0:1], mv[:, 0:1], ALU.mult)
            ts("pool", t1, s2, 1.0 / 1024.0, LN_EPS, ALU.mult, ALU.add)
            tt("pool", t1, t1, msq, ALU.subtract)
            tt("pool", rs, t1, SML[0:nr, 0:1], ALU.pow)
        if not DEBUG.get('ln_pool'):
            stt(tmp[0:nr, :], xin, mv[:, 0:1], gb[0:nr, :], ALU.subtract, ALU.mult)
        else:
            ts("pool", t1, mv[:, 0:1], -1.0, None, ALU.mult)
            ts("pool", tmp[0:nr, :], xin, 1.0, t1, ALU.mult, ALU.add)
            tt("pool", tmp[0:nr, :], tmp[0:nr, :], gb[0:nr, :], ALU.mult)
        stt(xout, tmp[0:nr, :], rs, bb[0:nr, :], ALU.mult, ALU.add)

    def ln_tile(i, nr, slot, tmp=None):
        gb = LNP[:, slot * 2048: slot * 2048 + 1024]
        bb = LNP[:, slot * 2048 + 1024: slot * 2048 + 2048]
        ln_rows(X3[0:nr, i, :], X3[0:nr, i, :], nr, gb, bb, LNT if tmp is None else tmp)
        to_xt(i, nr)
        if no_xt[0] and final_ln[0]:
            if i < 16:
                dma_out(yp[i * 128:(i + 1) * 128, :], X3[:, i, :])
            else:
                dma_out(ys[:, :], X3[0:NS, 16, :])
            out_done.add(i)

    def lnp_load(slot, g_d, b_d, row):
        dma_in(LNP[:, slot * 2048: slot * 2048 + 1024], g_d[row:row + 1, :].partition_broadcast(128))
        dma_in(LNP[:, slot * 2048 + 1024: slot * 2048 + 2048], b_d[row:row + 1, :].partition_broadcast(128))

    def x_acc(i, nr, yps, first):
        xt_ = X3[0:nr, i, :]
        if first:
            stt(xt_, xt_, float(ALPHA), yps, ALU.mult, ALU.add)
        else:
            tt("dve", xt_, xt_, yps, ALU.add)

    def arena3(base, a, b):
        return ARENA[:, base: base + a * b].rearrange("p (a b) -> p a b", a=a)

    def a_base(gi):
        return 0 if gi % 2 == 0 else 14336

    def f_base(gi):
        return 0 if gi % 2 == 0 else 12288

    def load_a_group(j, gi):
        base = a_base(gi)
        wload(arena3(base, 8, 1024), a_win[j, gi].rearrange("p (k m) -> p k m", k=8))
        wload(arena3(base + 8192, 2, 1024), a_wout[j, gi].rearrange("p (h m) -> p h m", h=2))

    def load_f_group(l, gi):
        c0, n = FFN_GROUPS[gi]
        base = f_base(gi)
        src = f_win[l].rearrange("p (k m) -> p k m", k=8)[:, :, c0 * 256:(c0 + n) * 256]
        wload(arena3(base, 8, n * 256), src)
        src2 = f_wout[l].rearrange("p (c m) -> p c m", c=NFC)[:, c0:c0 + n, :]
        wload(arena3(base + 8 * n * 256, n, 1024), src2)

    def load_b_part(j, part):
        if part == "u":
            wload(arena3(0, 8, 1024), b_win[j].rearrange("p (k m) -> p k m", k=8)[:, :, 0:1024])
        elif part == "v0":
            wload(arena3(8192, 8, 512), b_win[j].rearrange("p (k m) -> p k m", k=8)[:, :, 1024:1536])
        elif part == "v1":
            wload(arena3(12288, 8, 512), b_win[j].rearrange("p (k m) -> p k m", k=8)[:, :, 1536:2048])
        else:
            wload(arena3(16384, 8, 1024), b_wout[j].rearrange("p (k m) -> p k m", k=8))

    stages = STAGES[:nstages]

    def prefetch_next_stage(si, early):
        if si >= len(stages):
            return
        kind, l = stages[si][0], int(stages[si][1])
        if kind == "A":
            if early:
                load_a_group(l // 2, 0)
        elif kind == "F":
            if early:
                load_f_group(l, 0)
        else:
            if early:
                load_b_part(l // 2, "u")
                load_b_part(l // 2, "v0")
            else:
                load_b_part(l // 2, "v1")
                load_b_part(l // 2, "o")

    def setup():
        for i in range(16):
            dma_in(X3[:, i, :], xp[i * 128:(i + 1) * 128, :], q=("act" if i % 2 else "sp"))
        dma_in(X3[0:NS, 16, :], xs[:, :])
        dma_in(CM[:], cmask[:, :])
        dma_in(IDF[:], cident[:, :])
        dma_in(LBW[:, 0:32], lbT[:, :])
        memset("dve", SML[:, 0:2], -0.5)
        memset("dve", ONESF[:], 1.0)
        vcopy("dve", IDB[:], IDF[:])
        raw = LBW[:, 0:32].rearrange("p (h l) -> p h l", l=4)
        mx = LBW[:, 32:40]
        A("dve", lambda e: e.tensor_reduce(out=mx, in_=raw, axis=mybir.AxisListType.X, op=ALU.max), r=[raw], w=[mx])
        ex = LBW[:, 40:72].rearrange("p (h l) -> p h l", l=4)
        tt("dve", ex, raw, mx.unsqueeze(2).to_broadcast([128, 8, 4]), ALU.subtract)
        act(ex, ex, AF.Exp)
        sm = LBW[:, 72:80]
        A("dve", lambda e: e.tensor_reduce(out=sm, in_=ex, axis=mybir.AxisListType.X, op=ALU.add), r=[ex], w=[sm])
        rc = LBW[:, 80:88]
        A("dve", lambda e: e.reciprocal(out=rc, in_=sm), r=[sm], w=[rc])
        pr = LBW[:, 88:120].rearrange("p (h l) -> p h l", l=4)
        tt("dve", pr, ex, rc.unsqueeze(2).to_broadcast([128, 8, 4]), ALU.mult)
        cum = LBW[:, 120:128]
        lb = LBW[:, 128:144]
        tt("dve", lb[:, 0:8], pr[:, :, 0], pr[:, :, 0], ALU.subtract)
        tt("dve", cum, pr[:, :, 0], pr[:, :, 1], ALU.add)
        tt("dve", cum, cum, pr[:, :, 2], ALU.add)
        tt("dve", lb[:, 8:16], cum, pr[:, :, 0], ALU.subtract)
        ts("dve", COEF[:, 0:16], lb, 0.5, 0.5, ALU.mult, ALU.add)
        ts("dve", COEF[:, 16:32], lb, -0.5, 0.5, ALU.mult, ALU.add)
        ts("dve", COEF[:, 32:48], lb, 0.5, -0.5, ALU.mult, ALU.add)
        prefetch_next_stage(0, True)
        prefetch_next_stage(0, False)
        xt_convert([(i, 128) for i in range(16)] + [(16, NS)], cast_eng="dve")

    def ffn_stage(si, l):
        lnp_load(0, lnfg, lnfb, l)
        blocks = [(256 * b, 256, [(2 * b, 128), (2 * b + 1, 128)]) for b in range(8)] + [(SEQ, NS, [(16, NS)])]
        SG = [SCRF[:, k * 256:(k + 1) * 256] for k in range(4)]
        AT = [SCRB[:, k * 256:(k + 1) * 256] for k in range(6)]
        cnt = [0]
        ng = len(FFN_GROUPS)
        for gi, (c0, n) in enumerate(FFN_GROUPS):
            base = f_base(gi)
            win = arena3(base, 8, n * 256)
            wout = arena3(base + 8 * n * 256, n, 1024)
            wrelay()
            if gi + 1 < ng:
                load_f_group(l, gi + 1)
            else:
                prefetch_next_stage(si + 1, True)
            steps = [(bi, ci) for bi in range(len(blocks)) for ci in range(n)]
            slots = {}

            def emit_h(bi, ci):
                tok0, ntok, _ = blocks[bi]
                k = cnt[0]
                cnt[0] += 1
                slots[(bi, ci)] = k
                hb = ps[:, (k % 3) * 512:(k % 3) * 512 + 512]
                for gu in range(2):
                    for kc in range(8):
                        mm(hb[:, gu * 256: gu * 256 + ntok], win[:, kc, ci * 256 + gu * 128: ci * 256 + gu * 128 + 128],
                           XT3[:, kc, tok0:tok0 + ntok], kc == 0, kc == 7)
                sg = SG[k % 4][:, 0:ntok]
                act(sg, hb[:, 0:ntok], AF.Silu)
                tt("dve", AT[k % 6][:, 0:ntok], sg, hb[:, 256:256 + ntok], ALU.mult)

            def emit_y(bi, ci):
                tok0, ntok, tiles = blocks[bi]
                k = slots[(bi, ci)]
                last_g = gi == ng - 1
                if ci == n - 1 and last_g:
                    flush()
                for ti, (tile, nr) in enumerate(tiles):
                    yb = ps[0:nr, (3 + 2 * ti) * 512:(3 + 2 * ti) * 512 + 1024]
                    for hh in range(2):
                        mm(yb[:, hh * 512:(hh + 1) * 512], AT[k % 6][:, ti * 128: ti * 128 + nr],
                           wout[:, ci, hh * 512:(hh + 1) * 512], ci == 0, ci == n - 1)
                    if ci == n - 1:
                        x_acc(tile, nr, yb, gi == 0)
                        if last_g:
                            ln_tile(tile, nr, 0)

            SK = 2
            for k in range(len(steps) + SK):
                if k < len(steps):
                    emit_h(*steps[k])
                if k >= SK:
                    emit_y(*steps[k - SK])
        flush()
        prefetch_next_stage(si + 1, False)

    def b_stage(si, l):
        j = l // 2
        lnp_load(1, b_lng, b_lnb, j)
        lnp_load(0, lnmg, lnmb, l)
        Wu = arena3(0, 8, 1024)
        Wv = [arena3(8192, 8, 512), arena3(12288, 8, 512)]
        Wo = arena3(16384, 8, 1024)
        GV = SCRF[:, 0:1024]; VT = SCRF[:, 1024:2048]; VLN = SCRF[:, 2048:3072]
        GUTS = [[SCRF[:, 3072:3584], SCRF[:, 3584:4096]], [SCRF[:, 4096:4608], SCRF[:, 4608:5120]]]
        BIASR = SCRF[0:1, 5120:6144]
        CW = SCRF[0:NS, 6144:6152]; CBc = SCRF[0:NS, 6152:6160]
        GUS = SCRF[0:NS, 0:1024]
        VBS = [SCRB[:, 0:1024], SCRB[:, 3072:4096]]
        PRT = SCRB[:, 1024:2048].rearrange("p (g t) -> p g t", g=8)
        WST = SCRB[:, 2048:3072].rearrange("p (g t) -> p g t", g=8)
        lngb = LNP[:, 2048:3072]; lnbb = LNP[:, 3072:4096]
        vps = ps[:, 0:1024]
        ups = [ps[:, 1024:1536], ps[:, 1536:2048]]
        mps = ps[:, 2048:2560]
        yps = ps[:, 2560:3584]
        wload(WST, b_wsT[j].rearrange("p (g t) -> p g t", g=8))
        wrelay()
        tt("dve", WST, WST, TRILT.unsqueeze(1).to_broadcast([128, 8, 128]), ALU.mult)
        dma_in(BIASR, b_bias[j:j + 1, :])
        dma_in(CW, b_w00[j:j + 1, :].partition_broadcast(NS))
        dma_in(CBc, b_b0[j:j + 1, :].partition_broadcast(NS))

        def front(i):
            nr = 128 if i < 16 else NS
            tok0 = i * 128
            VB = VBS[i % 2]
            GUT = GUTS[i % 2]
            for hh in range(2):
                for kc in range(8):
                    mm(vps[0:nr, hh * 512:(hh + 1) * 512], XT3[:, kc, tok0:tok0 + nr], Wv[hh][:, kc, :], kc == 0, kc == 7)
            act(GV[0:nr, :], vps[0:nr, :], AF.Gelu)
            ln_rows(GV[0:nr, :], VLN[0:nr, :], nr, lngb, lnbb, VT)
            if i == 15:
                dma_out(cvp[j], VLN[:, :])
            if i == 16:
                dma_out(cvs[j], VLN[0:NS, :])
            if i < 16:
                acopy(VB[0:nr, :], VLN[0:nr, :])
                for half in range(2):
                    for g4 in range(4):
                        ch = half * 4 + g4
                        for kc in range(8):
                            mm(ups[half][:, g4 * 128: g4 * 128 + 128], Wu[:, kc, ch * 128:(ch + 1) * 128],
                               XT3[:, kc, tok0:tok0 + 128], kc == 0, kc == 7)
                    act(GUT[half], ups[half], AF.Gelu)
            else:
                u2 = ps[0:NS, 1024:2048]
                for hh in range(2):
                    for kc in range(8):
                        mm(u2[:, hh * 512:(hh + 1) * 512], XT3[:, kc, tok0:tok0 + NS], Wu[:, kc, hh * 512:(hh + 1) * 512], kc == 0, kc == 7)
                act(GUS, u2, AF.Gelu)

        def back(i):
            flush()
            nr = 128 if i < 16 else NS
            VB = VBS[i % 2]
            GUT = GUTS[i % 2]
            if i < 16:
                for half in range(2):
                    for g4 in range(4):
                        g = half * 4 + g4
                        mm(mps[:, g4 * 128:(g4 + 1) * 128], VB[:, g * 128:(g + 1) * 128], WST[:, g, :], True, False)
                        mm(mps[:, g4 * 128:(g4 + 1) * 128], ONESF[0:1, 0:128], BIASR[0:1, g * 128:(g + 1) * 128], False, True)
                    tt("dve", SCRB[:, 1024 + half * 512: 1024 + (half + 1) * 512], GUT[half], mps, ALU.mult)
            else:
                v3 = VLN[0:NS, :].rearrange("p (g d) -> p g d", g=8)
                m3 = VT[0:NS, :].rearrange("p (g d) -> p g d", g=8)
                tt("dve", m3, v3, CW.unsqueeze(2).to_broadcast([NS, 8, 128]), ALU.mult)
                tt("dve", m3, m3, CBc.unsqueeze(2).to_broadcast([NS, 8, 128]), ALU.add)
                tt("dve", VB[0:NS, :], GUS, VT[0:NS, :], ALU.mult)
                for g in range(8):
                    tr_b(pt3[:, g, 0:NS], VB[0:NS, g * 128:(g + 1) * 128], NS)
                acopy(PRT[:, :, 0:NS], pt3[:, :, 0:NS])
            for g in range(8):
                for hh in range(2):
                    mm(yps[0:nr, hh * 512:(hh + 1) * 512], PRT[:, g, 0:nr], Wo[:, g, hh * 512:(hh + 1) * 512], g == 0, g == 7)
            x_acc(i, nr, yps[0:nr, :], True)
            ln_tile(i, nr, 0, VT)

        for t in range(NTILE + 1):
            if t < NTILE:
                front(t)
            if t >= 1:
                back(t - 1)
        flush()
        prefetch_next_stage(si + 1, True)
        prefetch_next_stage(si + 1, False)

    def a_stage(si, l):
        j = l // 2
        lnp_load(0, lnmg, lnmb, l)
        dma_in(NG[:, 0:128], a_ng[j:j + 1, :].partition_broadcast(128))
        dma_in(NG[:, 128:256], a_ng[j:j + 1, :].partition_broadcast(128))
        L1 = LNP[:, 2048:4096]
        AX = ARENA[:, 10240:14336]
        TH = SCRF[:, 0:256]
        SQ = [SCRF[:, 256:512], L1[:, 0:256]]
        FSH = SCRF[:, 512:772]; FS = SCRF[:, 772:1028]; INJ = SCRF[:, 1028:1284]
        KK = [SCRF[:, 1284:1540], L1[:, 256:512]]
        PP = [SCRF[:, 1540:1796], SCRF[:, 1796:2052], L1[:, 512:768]]
        RR = [SCRF[:, 2052:2308], L1[:, 768:1024]]
        TQ = SCRF[:, 2308:2564]
        SGT = [SCRF[:, 2564:2820], L1[:, 1024:1280]]
        GN = [SCRF[:, 2820:3076], SCRF[:, 3076:3332]]; SST = SCRF[:, 3332:3588]; IPC = SCRF[:, 3588:3592]
        SSQ = SCRF[:, 3592:3594]; RST = SCRF[:, 3594:3596]; RTM = SCRF[:, 3596:3598]; JUNK = SCRF[:, 3600:3728]
        OSB = SCRF[:, 3728:3760]
        OSBF = [L1[:, 1280:1408], L1[:, 1408:1536]]
        S0 = [SCRF[:, 5120 + k * 128: 5248 + k * 128] for k in range(8)]
        SN = [SCRF[:, 4096 + k * 128: 4224 + k * 128] for k in range(8)]
        QM = [SCRB[:, 0:512], SCRB[:, 512:1024]]
        QP = SCRB[:, 1024:1280]; KHT = SCRB[:, 1280:1536]
        KTK = [SCRB[:, 1536:1792], SCRB[:, 1792:2048]]
        VB = [SCRB[:, 2048:2304], SCRB[:, 2304:2560], AX[:, 0:256]]
        ATM = [SCRB[:, 2560:2816], SCRB[:, 2816:3072]]
        SB = [SCRB[:, 3072:3328], SCRB[:, 3328:3584]]
        OG = SCRB[:, 3584:3840]; OGT = SCRB[:, 3840:4096]
        KD = SCRB[0:NS, 5120:6144].rearrange("p (b d) -> p b d", b=8)
        pqf = ps[:, 0:512]
        pvg = ps[:, 1024:1536]
        pdS = ps[:, 1536:1792]; pA = ps[:, 1792:2048]
        POH = [ps[:, 2048:2176], ps[:, 512:640]]
        po = ps[:, 2048:2304]
        yps = ps[:, 2560:3584]
        cA = COEF[:, 0:16]; cB = COEF[:, 16:32]; nB = COEF[:, 32:48]
        NCK = 128 // CH

        memset("dve", FSH, 0.0)
        memset("dve", FS, 0.0)
        memset("dve", INJ, 0.0)
        memset("dve", INJ[:, CH - 1:256:CH], 1.0)
        memset("dve", QM[0], 0.0)
        memset("dve", QM[1], 0.0)
        pstep = list(SCRF[:].ap[0])
        bstep = list(SCRB[:].ap[0])
        scrf_t = SCRF[:].tensor
        scrb_t = SCRB[:].tensor
        lstep = list(LNP[:].ap[0])
        lnp_t = LNP[:].tensor
        fsh_rev = bass.AP(scrf_t, 512 + 256, [pstep, [-1, 256]])
        inj_rev = bass.AP(scrf_t, 1028 + 255, [pstep, [-1, 256]])
        rr_rev = [bass.AP(scrf_t, 2052 + 255, [pstep, [-1, 256]]), bass.AP(lnp_t, 2048 + 768 + 255, [lstep, [-1, 256]])]

        for gi in range(DEBUG.get('ngroups', 4)):
            base = a_base(gi)
            win = ARENA[:, base: base + 8192].rearrange("p (k s m) -> p k s m", k=8, s=4)
            wout = arena3(base + 8192, 2, 1024)
            if gi + 1 < 4:
                load_a_group(j, gi + 1)
            else:
                prefetch_next_stage(si + 1, True)
            hcol = [j * 8 + 2 * gi, j * 8 + 2 * gi + 1]
            memset("dve", SST, 0.0)
            memset("dve", SB[0], 0.0)
            cur = [0]

            def p0(i):
                tok0 = i * 128
                for sec in range(2):
                    for hh in range(2):
                        for kc in range(8):
                            mm(pqf[:, sec * 256 + hh * 128: sec * 256 + hh * 128 + 128],
                               win[:, kc, sec, hh * 128:(hh + 1) * 128], XT3[:, kc, tok0:tok0 + 128], kc == 0, kc == 7)
                act(TH, pqf[:, 256:512], AF.Tanh, scale=0.5)
                act(SQ[i % 2], pqf[:, 0:256], AF.Silu)

            def p1(i):
                tok0 = i * 128
                for sec in range(2):
                    for kc in range(8):
                        mm(pvg[:, sec * 256:(sec + 1) * 256], XT3[:, kc, tok0:tok0 + 128], win[:, kc, 2 + sec, :], kc == 0, kc == 7)
                acopy(VB[i % 3], pvg[:, 0:256])
                act(SGT[i % 2], pvg[:, 256:512], AF.Silu)
                for hh in range(2):
                    c = hcol[hh]
                    ts("dve", FSH[:, hh * 128:(hh + 1) * 128], TH[:, hh * 128:(hh + 1) * 128], cB[:, c:c + 1], cA[:, c:c + 1], ALU.mult, ALU.add)
                    ts("pool", KK[i % 2][:, hh * 128:(hh + 1) * 128], TH[:, hh * 128:(hh + 1) * 128], nB[:, c:c + 1], cB[:, c:c + 1], ALU.mult, ALU.add)
                vcopy("dve", FS[:, 0:256:CH], FSH[:, 0:256:CH])
                memset("dve", FSH[:, 0:256:CH], 0.0)
                pp = PP[i % 3]
                A("dve", lambda e: e.tensor_tensor_scan(out=pp, data0=FSH[:, 0:256], data1=FS, initial=0.0, op0=ALU.mult, op1=ALU.add),
                  r=[FSH[:, 0:256], FS], w=[pp])
                rrv = rr_rev[i % 2]
                A("dve", lambda e: e.tensor_tensor_scan(out=rrv, data0=fsh_rev, data1=inj_rev, initial=0.0, op0=ALU.mult, op1=ALU.add),
                  r=[FSH[:, 1:257], INJ], w=[RR[i % 2]])

            def p2(i):
                pb = i % 2
                pp = PP[i % 3]
                qmd = bass.AP(scrb_t, pb * 512, [bstep, [256, 2], [128 + CH, NCK], [1, CH]])
                stt(TQ, SQ[pb], float(QSCALE), pp, ALU.mult, ALU.mult)
                A("dve", lambda e: e.reciprocal(out=IPC, in_=pp[:, CH - 1:256:CH]), r=[pp], w=[IPC])
                qeng = DEBUG.get('q_eng', 'pool')
                A(qeng, lambda e: e.tensor_copy(out=qmd, in_=TQ.rearrange("p (h c i) -> p h c i", h=2, c=NCK)), r=[TQ], w=[QM[pb]])
                tt(qeng, QP.rearrange("p (c i) -> p c i", i=CH), TQ.rearrange("p (c i) -> p c i", i=CH),
                   IPC.unsqueeze(2).to_broadcast([128, 2 * NCK, CH]), ALU.mult)
                tt("pool", KHT, KK[pb], RR[pb], ALU.mult)
                tt("pool", GN[pb], SGT[pb], NG[:, :], ALU.mult)

            def p3(i):
                pb = i % 2
                for hh in range(2):
                    tr_b(pt[:, hh * 128:(hh + 1) * 128], KHT[:, hh * 128:(hh + 1) * 128], 128)
                vcopy('dve', KTK[pb], pt[:, 0:256])
                for hh in range(2):
                    mm(pA[:, hh * 128:(hh + 1) * 128], KHT[:, hh * 128:(hh + 1) * 128], QP[:, hh * 128:(hh + 1) * 128], True, True)
                tt("dve", ATM[pb].rearrange("p (h t) -> p h t", h=2), pA.rearrange("p (h t) -> p h t", h=2),
                   MASKA.unsqueeze(1).to_broadcast([128, 2, 128]), ALU.mult)

            def fin1a(nr, o_src):
                for hh in range(2):
                    act(JUNK[0:nr, :], o_src[hh][0:nr, :], AF.Square, accum=SSQ[0:nr, hh:hh + 1])
                rstd_from(SSQ[0:nr, :], RST[0:nr, :], nr, 1.0 / 128.0, RMS_EPS, RTM[0:nr, :])

            def fin1b(nr, o_src, gn):
                for hh in range(2):
                    stt(OG[0:nr, hh * 128:(hh + 1) * 128], o_src[hh][0:nr, :], RST[0:nr, hh:hh + 1],
                        gn[0:nr, hh * 128:(hh + 1) * 128], ALU.mult, ALU.mult)

            def fin2(nr, tile, first, last):
                fin2a(nr)
                fin2b(nr, tile, first, last)

            def fin2b(nr, tile, first, last):
                x_acc(tile, nr, yps[0:nr, :], first)
                if last:
                    ln_tile(tile, nr, 0)

            def fin2a(nr):
                for hh in range(2):
                    tr_b(pt[:, 256 + hh * 128: 256 + hh * 128 + nr], OG[0:nr, hh * 128:(hh + 1) * 128], nr)
                if nr == 128:
                    vcopy("dve", OGT, pt[:, 256:512])
                else:
                    vcopy("dve", OGT.rearrange("p (h t) -> p h t", h=2)[:, :, 0:nr], pt[:, 256:512].rearrange("p (h t) -> p h t", h=2)[:, :, 0:nr])
                for hh in range(2):
                    for half in range(2):
                        mm(yps[0:nr, half * 512:(half + 1) * 512], OGT[:, hh * 128: hh * 128 + nr], wout[:, hh, half * 512:(half + 1) * 512], hh == 0, hh == 1)

            def chain(i, jc):
                pb = i % 2
                for hh in range(2):
                    c = cur[0]
                    mm(POH[hh], QM[pb][:, hh * 256 + jc * 128: hh * 256 + jc * 128 + 128],
                       SB[c][:, hh * 128:(hh + 1) * 128], False, jc == NCK - 1)
                    mm(pdS[:, hh * 128:(hh + 1) * 128], KTK[pb][jc * CH:(jc + 1) * CH, hh * 128:(hh + 1) * 128],
                       VB[i % 3][jc * CH:(jc + 1) * CH, hh * 128:(hh + 1) * 128], True, True)
                    pc = PP[i % 3][:, hh * 128 + jc * CH + CH - 1: hh * 128 + jc * CH + CH]
                    stt(SST[:, hh * 128:(hh + 1) * 128], SST[:, hh * 128:(hh + 1) * 128], pc, pdS[:, hh * 128:(hh + 1) * 128], ALU.mult, ALU.add)
                    acopy(SB[1 - c][:, hh * 128:(hh + 1) * 128], SST[:, hh * 128:(hh + 1) * 128])
                cur[0] = 1 - cur[0]

            NTL = DEBUG.get('ntiles', 16)
            p0(0); p1(0)
            if NTL > 1:
                p0(1); p1(1)
            p2(0); p3(0)
            for t in range(NTL):
                pb = t % 2
                for hh in range(2):
                    mm(POH[hh], ATM[pb][:, hh * 128:(hh + 1) * 128], VB[t % 3][:, hh * 128:(hh + 1) * 128], True, False)
                chain(t, 0)
                if t >= 1:
                    fin2a(128)
                if t + 1 < NTL:
                    p2(t + 1)
                if t >= 1:
                    fin2b(128, t - 1, gi == 0, gi == 3)
                if t + 2 < NTL:
                    p0(t + 2)
                flush()
                for jc in range(1, NCK):
                    chain(t, jc)
                for hh in range(2):
                    acopy(OSBF[hh], POH[hh])
                fin1a(128, OSBF)
                if t + 1 < NTL:
                    p3(t + 1)
                if t + 2 < NTL:
                    p1(t + 2)
                fin1b(128, OSBF, GN[pb])
            fin2(128, NTL - 1, gi == 0, gi == 3)
            flush()
            for hh in range(2):
                dma_out(sp_o[j, 2 * gi + hh], SST[:, hh * 128:(hh + 1) * 128])

            if DEBUG.get('nosample'):
                continue
            tok0 = SEQ
            for sec in range(2):
                for hh in range(2):
                    for kc in range(8):
                        mm(pqf[:, sec * 32 + hh * 16: sec * 32 + hh * 16 + 16], win[:, kc, sec, hh * 128:(hh + 1) * 128],
                           XT3[:, kc, tok0:tok0 + NS], kc == 0, kc == 7)
            for sec in range(2):
                for kc in range(8):
                    mm(pvg[0:NS, sec * 256:(sec + 1) * 256], XT3[:, kc, tok0:tok0 + NS], win[:, kc, 2 + sec, :], kc == 0, kc == 7)
            act(TH[:, 0:32], pqf[:, 32:64], AF.Tanh, scale=0.5)
            act(SQ[0][:, 0:32], pqf[:, 0:32], AF.Silu)
            acopy(VB[0][0:NS, :], pvg[0:NS, 0:256])
            act(SGT[0][0:NS, :], pvg[0:NS, 256:512], AF.Silu)
            FSS = SCRF[:, 3760:3792]
            KKS = SCRF[:, 3792:3824]
            QSS = SCRF[:, 3824:3856]
            for hh in range(2):
                c = hcol[hh]
                ts("dve", FSS[:, hh * 16:(hh + 1) * 16], TH[:, hh * 16:(hh + 1) * 16], cB[:, c:c + 1], cA[:, c:c + 1], ALU.mult, ALU.add)
                ts("pool", KKS[:, hh * 16:(hh + 1) * 16], TH[:, hh * 16:(hh + 1) * 16], nB[:, c:c + 1], cB[:, c:c + 1], ALU.mult, ALU.add)
            ts("dve", QSS, SQ[0][:, 0:32], float(QSCALE), None, ALU.mult)
            tt("pool", GN[0][0:NS, :], SGT[0][0:NS, :], NG[0:NS, :], ALU.mult)
            for hh in range(2):
                tr_f(pA[0:NS, hh * 128:(hh + 1) * 128], KKS[:, hh * 16:(hh + 1) * 16], 128)
            DI = L1[:, 1536:1792]
            QD = SCRB[:, 4608:5120].rearrange("p (h b m) -> p h b m", h=2, b=NS)
            SNB = [SCRB[:, 4096 + k * 128: 4224 + k * 128] for k in range(4)]
            memset("dve", DI, 0.0)
            memset("dve", DI[:, 0:256:17], 1.0)
            for hh in range(2):
                tt("dve", QD[:, hh], QSS[:, hh * 16:(hh + 1) * 16].unsqueeze(2).to_broadcast([128, NS, NS]),
                   DI.rearrange("p (b m) -> p b m", b=NS), ALU.mult)
            po2 = ps[:, 2304:2560]
            order = [(hh, bh, b8) for hh in range(2) for bh in range(2) for b8 in range(8)]
            NSL = 8

            def ld(kk):
                hh_, bh_, b8_ = order[kk]
                dma_in(S0[kk % 8], st[j, bh_ * 8 + b8_, 2 * gi + hh_])

            def kv(kk):
                hh_, bh_, b8_ = order[kk]
                if b8_ == 0:
                    tt("dve", KD, pA[0:NS, hh_ * 128:(hh_ + 1) * 128].unsqueeze(1).to_broadcast([NS, 8, 128]),
                       IDF[0:NS, bh_ * 8:(bh_ + 1) * 8].unsqueeze(2).to_broadcast([NS, 8, 128]), ALU.mult)
                dsl = pdS[:, (kk % 2) * 128:(kk % 2) * 128 + 128]
                mm(dsl, KD[:, b8_, :], VB[0][0:NS, hh_ * 128:(hh_ + 1) * 128], True, True)
                b_ = bh_ * 8 + b8_
                stt(SN[kk % NSL], S0[kk % 8], FSS[:, hh_ * 16 + b_: hh_ * 16 + b_ + 1], dsl, ALU.mult, ALU.add)
                dma_out(ss_o[j, b_, 2 * gi + hh_], SN[kk % NSL], q=("act" if kk % 2 else "sp"))
                acopy(SNB[kk % 4], SN[kk % NSL])

            def omm(kk):
                hh_, bh_, b8_ = order[kk]
                b_ = bh_ * 8 + b8_
                mm(po2[0:NS, hh_ * 128:(hh_ + 1) * 128], QD[:, hh_, b_, :], SNB[kk % 4], b_ == 0, b_ == NS - 1)

            for kk in range(7):
                ld(kk)
            LAG = 3
            for kk in range(len(order) + LAG):
                if kk < len(order):
                    if kk + 7 < len(order):
                        ld(kk + 7)
                    kv(kk)
                if kk >= LAG:
                    omm(kk - LAG)
            osrc = [po2[:, 0:128], po2[:, 128:256]]
            fin1a(NS, osrc)
            fin1b(NS, osrc, GN[0])
            fin2(NS, 16, gi == 0, gi == 3)
            flush()
        prefetch_next_stage(si + 1, False)

    setup()
    for si, name in enumerate(stages):
        l = int(name[1])
        no_xt[0] = (si == len(stages) - 1)
        final_ln[0] = no_xt[0] and name[0] == "F"
        if name[0] == "A":
            a_stage(si, l)
        elif name[0] == "F":
            ffn_stage(si, l)
        else:
            b_stage(si, l)
    flush()
    for i in range(16):
        if i not in out_done:
            dma_out(yp[i * 128:(i + 1) * 128, :], X3[:, i, :])
    if 16 not in out_done:
        dma_out(ys[:, :], X3[0:NS, 16, :])
    run_sched(nc, S, [], es.enter_context)
    es.close()
    return nc


def _shared_layouts(inp):
    f = lambda a: np.ascontiguousarray(np.asarray(a, dtype=np.float32))
    a_w_in = f(inp["a_w_in"]); a_w_out = f(inp["a_w_out"])
    b_w_in = f(inp["b_w_in"]); b_w_out = f(inp["b_w_out"]); b_w_s = f(inp["b_w_s"]); b_bias_s = f(inp["b_bias_s"])
    ffn_w_in = f(inp["ffn_w_in"]); ffn_w_out = f(inp["ffn_w_out"])
    sh = {}
    sh["lnmg"] = f(inp["ln_mix_g"]); sh["lnmb"] = f(inp["ln_mix_b"])
    sh["lnfg"] = f(inp["ln_ffn_g"]); sh["lnfb"] = f(inp["ln_ffn_b"])
    sh["lbT"] = f(f(inp["a_lb_raw"]).reshape(4, 8, 128).transpose(2, 1, 0).reshape(128, 32))
    sh["a_win"] = f(a_w_in.reshape(2, 8, 128, 4, 4, 2, 128).transpose(0, 4, 2, 1, 3, 5, 6).reshape(2, 4, 128, 8192))
    sh["a_wout"] = f(a_w_out.reshape(2, 4, 2, 128, 1024).transpose(0, 1, 3, 2, 4).reshape(2, 4, 128, 2048))
    sh["a_ng"] = f(inp["a_norm_g"])
    sh["b_win"] = f(b_w_in.reshape(2, 8, 128, 2048).transpose(0, 2, 1, 3).reshape(2, 128, 8 * 2048))
    sh["b_wout"] = f(b_w_out.reshape(2, 8, 128, 1024).transpose(0, 2, 1, 3).reshape(2, 128, 8 * 1024))
    sh["b_lng"] = f(inp["b_ln_g"]); sh["b_lnb"] = f(inp["b_ln_b"])
    sh["b_wsT"] = f(b_w_s.transpose(0, 3, 1, 2).reshape(2, 128, 1024))
    sh["b_bias"] = f(b_bias_s.reshape(2, 1024))
    sh["b_w00"] = f(b_w_s[:, :, 0, 0]); sh["b_b0"] = f(b_bias_s[:, :, 0])
    sh["f_win"] = f(ffn_w_in.reshape(4, 8, 128, 2, NFC, 128).transpose(0, 2, 1, 4, 3, 5).reshape(4, 128, 8 * 2 * DFF))
    sh["f_wout"] = f(ffn_w_out.reshape(4, NFC, 128, 1024).transpose(0, 2, 1, 3).reshape(4, 128, NFC * 1024))
    idx = np.arange(128)
    maska = ((idx[:, None] // CH) == (idx[None, :] // CH)) & (idx[:, None] <= idx[None, :])
    tril = idx[:, None] <= idx[None, :]
    sh["cmask"] = f(np.concatenate([maska, tril], axis=1))
    sh["cident"] = f(np.eye(128))
    return sh


def _core_inputs(inp, sh, c):
    f = lambda a: np.ascontiguousarray(np.asarray(a, dtype=np.float32))
    m = dict(sh)
    m["xp"] = f(np.asarray(inp["x_prompt"])[c])
    m["xs"] = f(np.asarray(inp["x_sample"])[c * NS:(c + 1) * NS, 0, :])
    m["st"] = f(np.asarray(inp["state_hgrn"])[:, c * NS:(c + 1) * NS])
    return m


def run_cores(inp, cores, nstages=8):
    nc = build_program(nstages)
    sh = _shared_layouts(inp)
    in_maps = [_core_inputs(inp, sh, c) for c in cores]
    res = run_bass_kernel_spmd(nc, in_maps, core_ids=list(range(len(cores))))
    return res.results


def kernel(**inputs):
    res = run_cores(inputs, list(range(NCORES)), 8)
    y_prompt = np.zeros((NCORES, SEQ, D), np.float32)
    y_sample = np.zeros((NCORES * NS, 1, D), np.float32)
    sp = np.zeros((2, NCORES, 8, 128, 128), np.float32)
    ss = np.zeros((2, NCORES * NS, 8, 128, 128), np.float32)
    cvp = np.zeros((2, NCORES, 128, D), np.float32)
    cvs = np.zeros((2, NCORES * NS, 1, D), np.float32)
    for c, r in enumerate(res):
        y_prompt[c] = r["yp"]
        y_sample[c * NS:(c + 1) * NS, 0] = r["ys"]
        sp[:, c] = r["sp"]
        ss[:, c * NS:(c + 1) * NS] = r["ss"]
        cvp[:, c] = r["cvp"]
        cvs[:, c * NS:(c + 1) * NS, 0] = r["cvs"]
    return (y_prompt, y_sample, sp, ss, cvp, cvs)
```

```python
import numpy as np
import concourse.bass as bass
import concourse.mybir as mybir
from concourse.bass_utils import run_bass_kernel_spmd

F32 = mybir.dt.float32
BF16 = mybir.dt.bfloat16
AF = mybir.ActivationFunctionType
ALU = mybir.AluOpType

NCORES = 8
D = 1024
SEQ = 2048
NS = 16
NT = SEQ + NS
NTILE = 17
DEPTH = 4
DFF = 2816
NFC = DFF // 128
ALPHA = (2 * DEPTH) ** 0.25
LN_EPS = 1e-5
RMS_EPS = 1e-6
QSCALE = 128 ** -0.5
SEM_ROT = 1 << 30


class Op:
    __slots__ = ("eng", "fn", "deps", "signal", "sem", "val", "is_dma", "idx", "clear")


def _ap_range(ap):
    t = ap.tensor
    dims = list(ap.ap)
    rowlen = dims[0][0]
    off = int(ap.offset)
    if rowlen <= 0:
        rowlen = 1 << 40
    plo = off // rowlen
    clo = off % rowlen
    phi = plo + dims[0][1]
    lo = clo
    hi = clo
    for st, cnt in dims[1:]:
        if st >= 0:
            hi += st * (cnt - 1)
        else:
            lo += st * (cnt - 1)
    return (t.name, plo, phi, lo, hi + 1)


class Sched:
    ENG = ("pe", "act", "dve", "pool", "sp")

    def __init__(self, nc):
        self.nc = nc
        self.ops = {e: [] for e in self.ENG}
        self.recs = {}
        self.nsig = {e: 0 for e in self.ENG}
        self.ndma = 0
        self.dma_last = {}
        self.NDMASEM = 16
        self.psum_bank = {"ps": 512, "pt": 1024}
        self.NSW = 8
        self.nsw = 0
        self.sw_pending = {}

    def add(self, eng, fn, reads=(), writes=(), dma=False, extra=(), sw=False, clear=()):
        op = Op()
        op.clear = list(clear)
        op.eng = eng
        op.fn = fn
        op.signal = False
        op.sem = None
        op.val = None
        op.is_dma = dma
        op.idx = len(self.ops[eng])
        deps = set(extra)
        for ap, isw in [(a, False) for a in reads] + [(a, True) for a in writes]:
            name, plo, phi, lo, hi = _ap_range(ap)
            bank = self.psum_bank.get(name)
            if bank is not None and isw and eng == "pe":
                lo = (lo // bank) * bank
                hi = -(-hi // bank) * bank
                plo, phi = 0, 128
            lst = self.recs.setdefault(name, [])
            keep = []
            for r in lst:
                ov = not (r[1] <= plo or phi <= r[0] or r[3] <= lo or hi <= r[2])
                if ov and (isw or r[5]):
                    if r[4] is not op:
                        deps.add(r[4])
                cov = plo <= r[0] and r[1] <= phi and lo <= r[2] and r[3] <= hi
                if cov and (isw or (not r[5] and r[4].eng == eng and not r[4].is_dma and not dma)):
                    continue
                keep.append(r)
            keep.append([plo, phi, lo, hi, op, isw])
            self.recs[name] = keep
        if dma and sw:
            op.sem = ("sw", self.nsw)
            self.nsw += 1
            op.val = 16
            op.signal = True
        elif dma:
            slot = self.ndma % self.NDMASEM
            self.ndma += 1
            prev = self.dma_last.get(slot)
            if prev is not None:
                deps.add(prev)
            self.dma_last[slot] = op
            op.sem = ("dma", slot)
            op.val = 16 * ((self.ndma - 1) // self.NDMASEM + 1)
            op.signal = True
        op.deps = [d for d in deps if not (d.eng == "pe" and eng == "pe" and not d.is_dma and not dma)]
        for d in op.deps:
            d.signal = True
        for key in op.clear:
            self.sw_pending.pop(key[1], None)
        self.ops[eng].append(op)
        return op


def run_sched(nc, S, final_ops, ctx_enter):
    last_ops = []
    for e in S.ENG:
        comp = [op for op in S.ops[e] if not op.is_dma]
        if comp:
            comp[-1].signal = True
            last_ops.append(comp[-1])
    sem_objs = {}
    for e in S.ENG:
        n = 0
        for op in S.ops[e]:
            if op.is_dma or not op.signal:
                continue
            key = (e, n // SEM_ROT)
            op.sem = key
            op.val = n % SEM_ROT + 1
            n += 1
            sem_objs.setdefault(key, None)
    for slot in range(min(S.ndma, S.NDMASEM)):
        sem_objs[("dma", slot)] = None
    for slot in range(S.nsw):
        sem_objs[("sw", slot)] = None
    for key in list(sem_objs):
        sem_objs[key] = ctx_enter(nc.semaphore("s_%s_%d" % key))
    def emit_engine(ename, eng):
        waited = {}
        for op in S.ops[ename]:
            best = {}
            for d in op.deps:
                if d.val is None:
                    continue
                if best.get(d.sem, 0) < d.val:
                    best[d.sem] = d.val
            for key, v in best.items():
                if waited.get(key, 0) >= v:
                    continue
                eng.wait_ge(sem_objs[key], v)
                waited[key] = v
                if DEBUG.get('trace'):
                    print('   ', ename, 'WAIT', key, v)
            for key in op.clear:
                eng.sem_clear(sem_objs[key])
                waited.pop(key, None)
            ins = op.fn(eng)
            if DEBUG.get('trace'):
                print(ename, op.idx, 'sig' if op.signal else '', op.sem, op.val, 'dma' if op.is_dma else '')
            if op.is_dma:
                ins.then_inc(sem_objs[op.sem], 16)
            elif op.signal:
                ins.then_inc(sem_objs[op.sem], 1)
        if ename == "sp":
            for slot, op in S.dma_last.items():
                eng.wait_ge(sem_objs[op.sem], op.val)
            for slot in range(S.nsw):
                eng.wait_ge(sem_objs[("sw", slot)], 16)
            for op in last_ops:
                eng.wait_ge(sem_objs[op.sem], op.val)

    block = ctx_enter(nc.Block())

    @block.tensor
    def _(eng):
        emit_engine("pe", eng)

    @block.scalar
    def _(eng):
        emit_engine("act", eng)

    @block.vector
    def _(eng):
        emit_engine("dve", eng)

    @block.gpsimd
    def _(eng):
        emit_engine("pool", eng)

    @block.sync
    def _(eng):
        emit_engine("sp", eng)


import contextlib

STAGES = ["A0", "F0", "B1", "F1", "A2", "F2", "B3", "F3"]
FFN_GROUPS = [(0, 2), (2, 4), (6, 4), (10, 4), (14, 4), (18, 4)]
CH = 64
DEBUG = {}


def build_program(nstages=8):
    nc = bass.Bass("TRN2", target_bir_lowering=False)
    es = contextlib.ExitStack()

    def din(name, shape):
        return nc.dram_tensor(name, list(shape), F32, kind="ExternalInput").ap()

    def dout(name, shape):
        return nc.dram_tensor(name, list(shape), F32, kind="ExternalOutput").ap()

    xp = din("xp", [SEQ, D]); xs = din("xs", [NS, D]); st = din("st", [2, NS, 8, 128, 128])
    lnmg = din("lnmg", [4, D]); lnmb = din("lnmb", [4, D]); lnfg = din("lnfg", [4, D]); lnfb = din("lnfb", [4, D])
    lbT = din("lbT", [128, 32])
    a_win = din("a_win", [2, 4, 128, 8192]); a_wout = din("a_wout", [2, 4, 128, 2048]); a_ng = din("a_ng", [2, 128])
    b_win = din("b_win", [2, 128, 8 * 2048]); b_wout = din("b_wout", [2, 128, 8 * 1024])
    b_lng = din("b_lng", [2, D]); b_lnb = din("b_lnb", [2, D]); b_wsT = din("b_wsT", [2, 128, 1024])
    b_bias = din("b_bias", [2, D]); b_w00 = din("b_w00", [2, 8]); b_b0 = din("b_b0", [2, 8])
    f_win = din("f_win", [4, 128, 8 * 2 * DFF]); f_wout = din("f_wout", [4, 128, NFC * 1024])
    cmask = din("cmask", [128, 256]); cident = din("cident", [128, 128])

    yp = dout("yp", [SEQ, D]); ys = dout("ys", [NS, D]); sp_o = dout("sp", [2, 8, 128, 128])
    ss_o = dout("ss", [2, NS, 8, 128, 128]); cvp = dout("cvp", [2, 128, D]); cvs = dout("cvs", [2, NS, D])

    def T(name, shape, dt):
        return es.enter_context(nc.sbuf_tensor(name, list(shape), dt))

    X = T("X", [128, NTILE * D], F32)
    XT = T("XT", [128, 8 * NT], BF16)
    ARENA = T("ARENA", [128, 24576], BF16)
    LNP = T("LNP", [128, 4096], F32)
    CM = T("CM", [128, 256], F32)
    IDF = T("IDF", [128, 128], F32)
    IDB = T("IDB", [128, 128], BF16)
    ONESF = T("ONESF", [128, 128], F32)
    NG = T("NG", [128, 256], F32)
    COEF = T("COEF", [128, 64], F32)
    LBW = T("LBW", [128, 160], F32)
    STAT = T("STAT", [128, 128], F32)
    SML = T("SML", [128, 64], F32)
    SCRF = T("SCRF", [128, 6400], F32)
    SCRB = T("SCRB", [128, 6144], BF16)
    ps = es.enter_context(nc.psum_tensor("ps", [128, 7 * 512], F32))
    pt = es.enter_context(nc.psum_tensor("pt", [128, 1024], BF16))

    X3 = X[:].rearrange("p (i d) -> p i d", d=D)
    XT3 = XT[:].rearrange("p (k t) -> p k t", k=8)
    pt3 = pt[:].rearrange("p (k t) -> p k t", k=8)
    NEGHALF = SML[:, 0:2]
    MASKA = CM[:, 0:128]
    TRILT = CM[:, 128:256]

    S = Sched(nc)

    def A(eng, fn, r=(), w=(), dma=False):
        return S.add(eng, fn, reads=r, writes=w, dma=dma)

    def mm(out, lhsT, rhs, start, stop):
        A("pe", lambda e: e.matmul(out, lhsT=lhsT, rhs=rhs, start=start, stop=stop), r=[lhsT, rhs], w=[out])

    def tr_b(out, in_, nr):
        idn = IDB[0:nr, 0:nr]
        A("pe", lambda e: e.transpose(out=out, in_=in_, identity=idn), r=[in_, idn], w=[out])

    def tr_f(out, in_, nr):
        idn = IDF[0:nr, 0:nr]
        A("pe", lambda e: e.transpose(out=out, in_=in_, identity=idn), r=[in_, idn], w=[out])

    def act(out, in_, func, scale=1.0, accum=None):
        if accum is None:
            A("act", lambda e: e.activation(out=out, in_=in_, func=func, scale=scale), r=[in_], w=[out])
        else:
            A("act", lambda e: e.activation(out=out, in_=in_, func=func, scale=scale, accum_out=accum), r=[in_], w=[out, accum])

    def acopy(out, in_):
        A("act", lambda e: e.copy(out=out, in_=in_), r=[in_], w=[out])

    def tt(eng, out, in0, in1, op):
        A(eng, lambda e: e.tensor_tensor(out=out, in0=in0, in1=in1, op=op), r=[in0, in1], w=[out])

    def ts(eng, out, in0, s1, s2, op0, op1=None):
        rs = [in0] + [s for s in (s1, s2) if not isinstance(s, (int, float)) and s is not None]
        if op1 is None:
            A(eng, lambda e: e.tensor_scalar(out=out, in0=in0, scalar1=s1, scalar2=None, op0=op0), r=rs, w=[out])
        else:
            A(eng, lambda e: e.tensor_scalar(out=out, in0=in0, scalar1=s1, scalar2=s2, op0=op0, op1=op1), r=rs, w=[out])

    def stt(out, in0, scalar, in1, op0, op1):
        rs = [in0, in1] + ([] if isinstance(scalar, (int, float)) else [scalar])
        A("dve", lambda e: e.scalar_tensor_tensor(out=out, in0=in0, scalar=scalar, in1=in1, op0=op0, op1=op1), r=rs, w=[out])

    def vcopy(eng, out, in_):
        A(eng, lambda e: e.tensor_copy(out=out, in_=in_), r=[in_], w=[out])

    def memset(eng, ap, val):
        A(eng, lambda e: e.memset(ap, val), w=[ap])

    def dma_in(out, in_, q="sp"):
        return A(q, lambda e: e.dma_start(out=out, in_=in_), w=[out], dma=True)

    def dma_out(out, in_, q="sp"):
        return A(q, lambda e: e.dma_start(out=out, in_=in_), r=[in_], dma=True)

    def wload(dst, src):
        return S.add("pool", lambda e: e.dma_start(out=dst, in_=src, max_dma_last_dim=4096), writes=[dst], dma=True, sw=True)

    def wrelay():
        pass

    pend = []
    XBS = [SCRB[:, 4096:5120], SCRB[:, 5120:6144]]
    xb_i = [0]

    def xt_convert(items, cast_eng="act"):
        for g0 in range(0, len(items), 2):
            grp = items[g0:g0 + 2]
            bufs = []
            for (i, nr) in grp:
                xb = XBS[xb_i[0] % 2]
                xb_i[0] += 1
                bufs.append(xb)
                if cast_eng == "act":
                    acopy(xb[0:nr, :], X3[0:nr, i, :])
                else:
                    vcopy(cast_eng, xb[0:nr, :], X3[0:nr, i, :])
            for (i, nr), xb in zip(grp, bufs):
                tok0 = i * 128
                for kc in range(8):
                    tr_b(pt3[:, kc, 0:nr], xb[0:nr, kc * 128:(kc + 1) * 128], nr)
                acopy(XT3[:, :, tok0:tok0 + nr], pt3[:, :, 0:nr])

    def flush():
        while pend:
            i, nr, xb = pend.pop(0)
            tok0 = i * 128
            for kc in range(8):
                tr_b(pt3[:, kc, 0:nr], xb[0:nr, kc * 128:(kc + 1) * 128], nr)
            acopy(XT3[:, :, tok0:tok0 + nr], pt3[:, :, 0:nr])

    stat_i = [0]

    def rstd_from(var_ap, out_ap, nr, scale, eps, tmp_ap):
        ts("pool", tmp_ap, var_ap, scale, eps, ALU.mult, ALU.add)
        k = out_ap.shape[1]
        tt("pool", out_ap, tmp_ap, SML[0:nr, 0:k], ALU.pow)

    no_xt = [False]
    final_ln = [False]
    out_done = set()

    def flush_one():
        if pend:
            i, nr, xb = pend.pop(0)
            tok0 = i * 128
            for kc in range(8):
                tr_b(pt3[:, kc, 0:nr], xb[0:nr, kc * 128:(kc + 1) * 128], nr)
            acopy(XT3[:, :, tok0:tok0 + nr], pt3[:, :, 0:nr])

    def to_xt(i, nr):
        if not no_xt[0]:
            assert len(pend) < 2, "at most two staged tiles (two staging buffers)"
            xb = XBS[xb_i[0] % 2]
            xb_i[0] += 1
            acopy(xb[0:nr, :], X3[0:nr, i, :])
            pend.append((i, nr, xb))

    LNT = SCRF[:, 4096:5120]

    def ln_rows(xin, xout, nr, gb, bb, tmp):
        sl = stat_i[0] % 8
        stat_i[0] += 1
        st6 = STAT[0:nr, sl * 16: sl * 16 + 12]
        mv = STAT[0:nr, sl * 16 + 12: sl * 16 + 14]
        rs = STAT[0:nr, sl * 16 + 14: sl * 16 + 15]
        t1 = STAT[0:nr, sl * 16 + 15: sl * 16 + 16]
        if DEBUG.get('ln_bn', True):
            A("dve", lambda e: e.bn_stats(out=st6[:, 0:6], in_=xin[:, 0:512]), r=[xin[:, 0:512]], w=[st6[:, 0:6]])
            A("dve", lambda e: e.bn_stats(out=st6[:, 6:12], in_=xin[:, 512:1024]), r=[xin[:, 512:1024]], w=[st6[:, 6:12]])
            A("dve", lambda e: e.bn_aggr(out=mv, in_=st6), r=[st6], w=[mv])
            rstd_from(mv[:, 1:2], rs, nr, 1.0, LN_EPS, t1)
        else:
            s1 = st6[:, 0:1]; s2 = st6[:, 1:2]; msq = st6[:, 2:3]
            act(tmp[0:nr, :], xin, AF.Identity, accum=s1)
            act(tmp[0:nr, :], xin, AF.Square, accum=s2)
            ts("pool", mv[:, 0:1], s1, 1.0 / 1024.0, None, ALU.mult)
            tt("pool", msq, mv[:, 0:1], mv[:, 0:1], ALU.mult)
            ts("pool", t1, s2, 1.0 / 1024.0, LN_EPS, ALU.mult, ALU.add)
            tt("pool", t1, t1, msq, ALU.subtract)
            tt("pool", rs, t1, SML[0:nr, 0:1], ALU.pow)
        if not DEBUG.get('ln_pool'):
            stt(tmp[0:nr, :], xin, mv[:, 0:1], gb[0:nr, :], ALU.subtract, ALU.mult)
        else:
            ts("pool", t1, mv[:, 0:1], -1.0, None, ALU.mult)
            ts("pool", tmp[0:nr, :], xin, 1.0, t1, ALU.mult, ALU.add)
            tt("pool", tmp[0:nr, :], tmp[0:nr, :], gb[0:nr, :], ALU.mult)
        stt(xout, tmp[0:nr, :], rs, bb[0:nr, :], ALU.mult, ALU.add)

    def ln_tile(i, nr, slot, tmp=None):
        gb = LNP[:, slot * 2048: slot * 2048 + 1024]
        bb = LNP[:, slot * 2048 + 1024: slot * 2048 + 2048]
        ln_rows(X3[0:nr, i, :], X3[0:nr, i, :], nr, gb, bb, LNT if tmp is None else tmp)
        to_xt(i, nr)
        if no_xt[0] and final_ln[0]:
            if i < 16:
                dma_out(yp[i * 128:(i + 1) * 128, :], X3[:, i, :])
            else:
                dma_out(ys[:, :], X3[0:NS, 16, :])
            out_done.add(i)

    def lnp_load(slot, g_d, b_d, row):
        dma_in(LNP[:, slot * 2048: slot * 2048 + 1024], g_d[row:row + 1, :].partition_broadcast(128))
        dma_in(LNP[:, slot * 2048 + 1024: slot * 2048 + 2048], b_d[row:row + 1, :].partition_broadcast(128))

    def x_acc(i, nr, yps, first):
        xt_ = X3[0:nr, i, :]
        if first:
            stt(xt_, xt_, float(ALPHA), yps, ALU.mult, ALU.add)
        else:
            tt("dve", xt_, xt_, yps, ALU.add)

    def arena3(base, a, b):
        return ARENA[:, base: base + a * b].rearrange("p (a b) -> p a b", a=a)

    def a_base(gi):
        return 0 if gi % 2 == 0 else 14336

    def f_base(gi):
        return 0 if gi % 2 == 0 else 12288

    def load_a_group(j, gi):
        base = a_base(gi)
        wload(arena3(base, 8, 1024), a_win[j, gi].rearrange("p (k m) -> p k m", k=8))
        wload(arena3(base + 8192, 2, 1024), a_wout[j, gi].rearrange("p (h m) -> p h m", h=2))

    def load_f_group(l, gi):
        c0, n = FFN_GROUPS[gi]
        base = f_base(gi)
        src = f_win[l].rearrange("p (k m) -> p k m", k=8)[:, :, c0 * 256:(c0 + n) * 256]
        wload(arena3(base, 8, n * 256), src)
        src2 = f_wout[l].rearrange("p (c m) -> p c m", c=NFC)[:, c0:c0 + n, :]
        wload(arena3(base + 8 * n * 256, n, 1024), src2)

    def load_b_part(j, part):
        if part == "u":
            wload(arena3(0, 8, 1024), b_win[j].rearrange("p (k m) -> p k m", k=8)[:, :, 0:1024])
        elif part == "v0":
            wload(arena3(8192, 8, 512), b_win[j].rearrange("p (k m) -> p k m", k=8)[:, :, 1024:1536])
        elif part == "v1":
            wload(arena3(12288, 8, 512), b_win[j].rearrange("p (k m) -> p k m", k=8)[:, :, 1536:2048])
        else:
            wload(arena3(16384, 8, 1024), b_wout[j].rearrange("p (k m) -> p k m", k=8))

    stages = STAGES[:nstages]

    def prefetch_next_stage(si, early):
        if si >= len(stages):
            return
        kind, l = stages[si][0], int(stages[si][1])
        if kind == "A":
            if early:
                load_a_group(l // 2, 0)
        elif kind == "F":
            if early:
                load_f_group(l, 0)
        else:
            if early:
                load_b_part(l // 2, "u")
                load_b_part(l // 2, "v0")
            else:
                load_b_part(l // 2, "v1")
                load_b_part(l // 2, "o")

    def setup():
        for i in range(16):
            dma_in(X3[:, i, :], xp[i * 128:(i + 1) * 128, :], q=("act" if i % 2 else "sp"))
        dma_in(X3[0:NS, 16, :], xs[:, :])
        dma_in(CM[:], cmask[:, :])
        dma_in(IDF[:], cident[:, :])
        dma_in(LBW[:, 0:32], lbT[:, :])
        memset("dve", SML[:, 0:2], -0.5)
        memset("dve", ONESF[:], 1.0)
        vcopy("dve", IDB[:], IDF[:])
        raw = LBW[:, 0:32].rearrange("p (h l) -> p h l", l=4)
        mx = LBW[:, 32:40]
        A("dve", lambda e: e.tensor_reduce(out=mx, in_=raw, axis=mybir.AxisListType.X, op=ALU.max), r=[raw], w=[mx])
        ex = LBW[:, 40:72].rearrange("p (h l) -> p h l", l=4)
        tt("dve", ex, raw, mx.unsqueeze(2).to_broadcast([128, 8, 4]), ALU.subtract)
        act(ex, ex, AF.Exp)
        sm = LBW[:, 72:80]
        A("dve", lambda e: e.tensor_reduce(out=sm, in_=ex, axis=mybir.AxisListType.X, op=ALU.add), r=[ex], w=[sm])
        rc = LBW[:, 80:88]
        A("dve", lambda e: e.reciprocal(out=rc, in_=sm), r=[sm], w=[rc])
        pr = LBW[:, 88:120].rearrange("p (h l) -> p h l", l=4)
        tt("dve", pr, ex, rc.unsqueeze(2).to_broadcast([128, 8, 4]), ALU.mult)
        cum = LBW[:, 120:128]
        lb = LBW[:, 128:144]
        tt("dve", lb[:, 0:8], pr[:, :, 0], pr[:, :, 0], ALU.subtract)
        tt("dve", cum, pr[:, :, 0], pr[:, :, 1], ALU.add)
        tt("dve", cum, cum, pr[:, :, 2], ALU.add)
        tt("dve", lb[:, 8:16], cum, pr[:, :, 0], ALU.subtract)
        ts("dve", COEF[:, 0:16], lb, 0.5, 0.5, ALU.mult, ALU.add)
        ts("dve", COEF[:, 16:32], lb, -0.5, 0.5, ALU.mult, ALU.add)
        ts("dve", COEF[:, 32:48], lb, 0.5, -0.5, ALU.mult, ALU.add)
        prefetch_next_stage(0, True)
        prefetch_next_stage(0, False)
        xt_convert([(i, 128) for i in range(16)] + [(16, NS)], cast_eng="dve")

    def ffn_stage(si, l):
        lnp_load(0, lnfg, lnfb, l)
        blocks = [(256 * b, 256, [(2 * b, 128), (2 * b + 1, 128)]) for b in range(8)] + [(SEQ, NS, [(16, NS)])]
        SG = [SCRF[:, k * 256:(k + 1) * 256] for k in range(4)]
        AT = [SCRB[:, k * 256:(k + 1) * 256] for k in range(6)]
        cnt = [0]
        ng = len(FFN_GROUPS)
        for gi, (c0, n) in enumerate(FFN_GROUPS):
            base = f_base(gi)
            win = arena3(base, 8, n * 256)
            wout = arena3(base + 8 * n * 256, n, 1024)
            wrelay()
            if gi + 1 < ng:
                load_f_group(l, gi + 1)
            else:
                prefetch_next_stage(si + 1, True)
            steps = [(bi, ci) for bi in range(len(blocks)) for ci in range(n)]
            slots = {}

            def emit_h(bi, ci):
                tok0, ntok, _ = blocks[bi]
                k = cnt[0]
                cnt[0] += 1
                slots[(bi, ci)] = k
                hb = ps[:, (k % 3) * 512:(k % 3) * 512 + 512]
                for gu in range(2):
                    for kc in range(8):
                        mm(hb[:, gu * 256: gu * 256 + ntok], win[:, kc, ci * 256 + gu * 128: ci * 256 + gu * 128 + 128],
                           XT3[:, kc, tok0:tok0 + ntok], kc == 0, kc == 7)
                sg = SG[k % 4][:, 0:ntok]
                act(sg, hb[:, 0:ntok], AF.Silu)
                tt("dve", AT[k % 6][:, 0:ntok], sg, hb[:, 256:256 + ntok], ALU.mult)

            def emit_y(bi, ci):
                tok0, ntok, tiles = blocks[bi]
                k = slots[(bi, ci)]
                last_g = gi == ng - 1
                if last_g and ci >= n - 2:
                    if ci == n - 1:
                        flush()
                    else:
                        flush_one()
                for ti, (tile, nr) in enumerate(tiles):
                    yb = ps[0:nr, (3 + 2 * ti) * 512:(3 + 2 * ti) * 512 + 1024]
                    for hh in range(2):
                        mm(yb[:, hh * 512:(hh + 1) * 512], AT[k % 6][:, ti * 128: ti * 128 + nr],
                           wout[:, ci, hh * 512:(hh + 1) * 512], ci == 0, ci == n - 1)
                    if ci == n - 1:
                        x_acc(tile, nr, yb, gi == 0)
                        if last_g:
                            ln_tile(tile, nr, 0)

            SK = 2
            for k in range(len(steps) + SK):
                if k < len(steps):
                    emit_h(*steps[k])
                if k >= SK:
                    emit_y(*steps[k - SK])
        flush()
        prefetch_next_stage(si + 1, False)

    def b_stage(si, l):
        j = l // 2
        lnp_load(1, b_lng, b_lnb, j)
        lnp_load(0, lnmg, lnmb, l)
        Wu = arena3(0, 8, 1024)
        Wv = [arena3(8192, 8, 512), arena3(12288, 8, 512)]
        Wo = arena3(16384, 8, 1024)
        GV = SCRF[:, 0:1024]; VT = SCRF[:, 1024:2048]; VLN = SCRF[:, 2048:3072]
        GUTS = [[SCRF[:, 3072:3584], SCRF[:, 3584:4096]], [SCRF[:, 4096:4608], SCRF[:, 4608:5120]]]
        BIASR = SCRF[0:1, 5120:6144]
        CW = SCRF[0:NS, 6144:6152]; CBc = SCRF[0:NS, 6152:6160]
        GUS = SCRF[0:NS, 0:1024]
        VBS = [SCRB[:, 0:1024], SCRB[:, 3072:4096]]
        PRT = SCRB[:, 1024:2048].rearrange("p (g t) -> p g t", g=8)
        WST = SCRB[:, 2048:3072].rearrange("p (g t) -> p g t", g=8)
        lngb = LNP[:, 2048:3072]; lnbb = LNP[:, 3072:4096]
        vps = ps[:, 0:1024]
        ups = [ps[:, 1024:1536], ps[:, 1536:2048]]
        mps = ps[:, 2048:2560]
        yps = ps[:, 2560:3584]
        wload(WST, b_wsT[j].rearrange("p (g t) -> p g t", g=8))
        wrelay()
        tt("dve", WST, WST, TRILT.unsqueeze(1).to_broadcast([128, 8, 128]), ALU.mult)
        dma_in(BIASR, b_bias[j:j + 1, :])
        dma_in(CW, b_w00[j:j + 1, :].partition_broadcast(NS))
        dma_in(CBc, b_b0[j:j + 1, :].partition_broadcast(NS))

        def front(i):
            nr = 128 if i < 16 else NS
            tok0 = i * 128
            VB = VBS[i % 2]
            GUT = GUTS[i % 2]
            for hh in range(2):
                for kc in range(8):
                    mm(vps[0:nr, hh * 512:(hh + 1) * 512], XT3[:, kc, tok0:tok0 + nr], Wv[hh][:, kc, :], kc == 0, kc == 7)
            act(GV[0:nr, :], vps[0:nr, :], AF.Gelu)
            ln_rows(GV[0:nr, :], VLN[0:nr, :], nr, lngb, lnbb, VT)
            if i == 15:
                dma_out(cvp[j], VLN[:, :])
            if i == 16:
                dma_out(cvs[j], VLN[0:NS, :])
            if i < 16:
                acopy(VB[0:nr, :], VLN[0:nr, :])
                for half in range(2):
                    for g4 in range(4):
                        ch = half * 4 + g4
                        for kc in range(8):
                            mm(ups[half][:, g4 * 128: g4 * 128 + 128], Wu[:, kc, ch * 128:(ch + 1) * 128],
                               XT3[:, kc, tok0:tok0 + 128], kc == 0, kc == 7)
                    act(GUT[half], ups[half], AF.Gelu)
            else:
                u2 = ps[0:NS, 1024:2048]
                for hh in range(2):
                    for kc in range(8):
                        mm(u2[:, hh * 512:(hh + 1) * 512], XT3[:, kc, tok0:tok0 + NS], Wu[:, kc, hh * 512:(hh + 1) * 512], kc == 0, kc == 7)
                act(GUS, u2, AF.Gelu)

        def back(i):
            flush()
            nr = 128 if i < 16 else NS
            VB = VBS[i % 2]
            GUT = GUTS[i % 2]
            if i < 16:
                for half in range(2):
                    for g4 in range(4):
                        g = half * 4 + g4
                        mm(mps[:, g4 * 128:(g4 + 1) * 128], VB[:, g * 128:(g + 1) * 128], WST[:, g, :], True, False)
                        mm(mps[:, g4 * 128:(g4 + 1) * 128], ONESF[0:1, 0:128], BIASR[0:1, g * 128:(g + 1) * 128], False, True)
                    tt("dve", SCRB[:, 1024 + half * 512: 1024 + (half + 1) * 512], GUT[half], mps, ALU.mult)
            else:
                v3 = VLN[0:NS, :].rearrange("p (g d) -> p g d", g=8)
                m3 = VT[0:NS, :].rearrange("p (g d) -> p g d", g=8)
                tt("dve", m3, v3, CW.unsqueeze(2).to_broadcast([NS, 8, 128]), ALU.mult)
                tt("dve", m3, m3, CBc.unsqueeze(2).to_broadcast([NS, 8, 128]), ALU.add)
                tt("dve", VB[0:NS, :], GUS, VT[0:NS, :], ALU.mult)
                for g in range(8):
                    tr_b(pt3[:, g, 0:NS], VB[0:NS, g * 128:(g + 1) * 128], NS)
                acopy(PRT[:, :, 0:NS], pt3[:, :, 0:NS])
            for g in range(8):
                for hh in range(2):
                    mm(yps[0:nr, hh * 512:(hh + 1) * 512], PRT[:, g, 0:nr], Wo[:, g, hh * 512:(hh + 1) * 512], g == 0, g == 7)
            x_acc(i, nr, yps[0:nr, :], True)
            ln_tile(i, nr, 0, VT)

        for t in range(NTILE + 1):
            if t < NTILE:
                front(t)
            if t >= 1:
                back(t - 1)
        flush()
        prefetch_next_stage(si + 1, True)
        prefetch_next_stage(si + 1, False)

    def a_stage(si, l):
        j = l // 2
        lnp_load(0, lnmg, lnmb, l)
        dma_in(NG[:, 0:128], a_ng[j:j + 1, :].partition_broadcast(128))
        dma_in(NG[:, 128:256], a_ng[j:j + 1, :].partition_broadcast(128))
        L1 = LNP[:, 2048:4096]
        AX = ARENA[:, 10240:14336]
        TH = SCRF[:, 0:256]
        SQ = [SCRF[:, 256:512], L1[:, 0:256]]
        FSH = SCRF[:, 512:772]; FS = SCRF[:, 772:1028]; INJ = SCRF[:, 1028:1284]
        KK = [SCRF[:, 1284:1540], L1[:, 256:512]]
        PP = [SCRF[:, 1540:1796], SCRF[:, 1796:2052], L1[:, 512:768]]
        RR = [SCRF[:, 2052:2308], L1[:, 768:1024]]
        TQ = SCRF[:, 2308:2564]
        SGT = [SCRF[:, 2564:2820], L1[:, 1024:1280]]
        GN = [SCRF[:, 2820:3076], SCRF[:, 3076:3332]]; SST = SCRF[:, 3332:3588]; IPC = SCRF[:, 3588:3592]
        SSQ = SCRF[:, 3592:3594]; RST = SCRF[:, 3594:3596]; RTM = SCRF[:, 3596:3598]; JUNK = SCRF[:, 3600:3728]
        OSB = SCRF[:, 3728:3760]
        OSBF = [L1[:, 1280:1408], L1[:, 1408:1536]]
        S0 = [SCRF[:, 5120 + k * 128: 5248 + k * 128] for k in range(8)]
        SN = [SCRF[:, 4096 + k * 128: 4224 + k * 128] for k in range(8)]
        QM = [SCRB[:, 0:512], SCRB[:, 512:1024]]
        QP = SCRB[:, 1024:1280]; KHT = SCRB[:, 1280:1536]
        KTK = [SCRB[:, 1536:1792], SCRB[:, 1792:2048]]
        VB = [SCRB[:, 2048:2304], SCRB[:, 2304:2560], AX[:, 0:256]]
        ATM = [SCRB[:, 2560:2816], SCRB[:, 2816:3072]]
        SB = [SCRB[:, 3072:3328], SCRB[:, 3328:3584]]
        OG = SCRB[:, 3584:3840]; OGT = SCRB[:, 3840:4096]
        KD = SCRB[0:NS, 5120:6144].rearrange("p (b d) -> p b d", b=8)
        pqf = ps[:, 0:512]
        pvg = ps[:, 1024:1536]
        pdS = ps[:, 1536:1792]; pA = ps[:, 1792:2048]
        POH = [ps[:, 2048:2176], ps[:, 512:640]]
        po = ps[:, 2048:2304]
        yps = ps[:, 2560:3584]
        cA = COEF[:, 0:16]; cB = COEF[:, 16:32]; nB = COEF[:, 32:48]
        NCK = 128 // CH

        memset("dve", FSH, 0.0)
        memset("dve", FS, 0.0)
        memset("dve", INJ, 0.0)
        memset("dve", INJ[:, CH - 1:256:CH], 1.0)
        memset("dve", QM[0], 0.0)
        memset("dve", QM[1], 0.0)
        pstep = list(SCRF[:].ap[0])
        bstep = list(SCRB[:].ap[0])
        scrf_t = SCRF[:].tensor
        scrb_t = SCRB[:].tensor
        lstep = list(LNP[:].ap[0])
        lnp_t = LNP[:].tensor
        fsh_rev = bass.AP(scrf_t, 512 + 256, [pstep, [-1, 256]])
        inj_rev = bass.AP(scrf_t, 1028 + 255, [pstep, [-1, 256]])
        rr_rev = [bass.AP(scrf_t, 2052 + 255, [pstep, [-1, 256]]), bass.AP(lnp_t, 2048 + 768 + 255, [lstep, [-1, 256]])]

        for gi in range(DEBUG.get('ngroups', 4)):
            base = a_base(gi)
            win = ARENA[:, base: base + 8192].rearrange("p (k s m) -> p k s m", k=8, s=4)
            wout = arena3(base + 8192, 2, 1024)
            if gi + 1 < 4:
                load_a_group(j, gi + 1)
            else:
                prefetch_next_stage(si + 1, True)
            hcol = [j * 8 + 2 * gi, j * 8 + 2 * gi + 1]
            memset("dve", SST, 0.0)
            memset("dve", SB[0], 0.0)
            cur = [0]

            def p0(i):
                tok0 = i * 128
                for sec in range(2):
                    for hh in range(2):
                        for kc in range(8):
                            mm(pqf[:, sec * 256 + hh * 128: sec * 256 + hh * 128 + 128],
                               win[:, kc, sec, hh * 128:(hh + 1) * 128], XT3[:, kc, tok0:tok0 + 128], kc == 0, kc == 7)
                act(TH, pqf[:, 256:512], AF.Tanh, scale=0.5)
                act(SQ[i % 2], pqf[:, 0:256], AF.Silu)

            def p1(i):
                tok0 = i * 128
                for sec in range(2):
                    for kc in range(8):
                        mm(pvg[:, sec * 256:(sec + 1) * 256], XT3[:, kc, tok0:tok0 + 128], win[:, kc, 2 + sec, :], kc == 0, kc == 7)
                acopy(VB[i % 3], pvg[:, 0:256])
                act(SGT[i % 2], pvg[:, 256:512], AF.Silu)
                for hh in range(2):
                    c = hcol[hh]
                    ts("dve", FSH[:, hh * 128:(hh + 1) * 128], TH[:, hh * 128:(hh + 1) * 128], cB[:, c:c + 1], cA[:, c:c + 1], ALU.mult, ALU.add)
                    ts("pool", KK[i % 2][:, hh * 128:(hh + 1) * 128], TH[:, hh * 128:(hh + 1) * 128], nB[:, c:c + 1], cB[:, c:c + 1], ALU.mult, ALU.add)
                vcopy("dve", FS[:, 0:256:CH], FSH[:, 0:256:CH])
                memset("dve", FSH[:, 0:256:CH], 0.0)
                pp = PP[i % 3]
                A("dve", lambda e: e.tensor_tensor_scan(out=pp, data0=FSH[:, 0:256], data1=FS, initial=0.0, op0=ALU.mult, op1=ALU.add),
                  r=[FSH[:, 0:256], FS], w=[pp])
                rrv = rr_rev[i % 2]
                A("dve", lambda e: e.tensor_tensor_scan(out=rrv, data0=fsh_rev, data1=inj_rev, initial=0.0, op0=ALU.mult, op1=ALU.add),
                  r=[FSH[:, 1:257], INJ], w=[RR[i % 2]])

            def p2(i):
                pb = i % 2
                pp = PP[i % 3]
                qmd = bass.AP(scrb_t, pb * 512, [bstep, [256, 2], [128 + CH, NCK], [1, CH]])
                stt(TQ, SQ[pb], float(QSCALE), pp, ALU.mult, ALU.mult)
                A("dve", lambda e: e.reciprocal(out=IPC, in_=pp[:, CH - 1:256:CH]), r=[pp], w=[IPC])
                qeng = DEBUG.get('q_eng', 'dve')
                A(qeng, lambda e: e.tensor_copy(out=qmd, in_=TQ.rearrange("p (h c i) -> p h c i", h=2, c=NCK)), r=[TQ], w=[QM[pb]])
                tt(qeng, QP.rearrange("p (c i) -> p c i", i=CH), TQ.rearrange("p (c i) -> p c i", i=CH),
                   IPC.unsqueeze(2).to_broadcast([128, 2 * NCK, CH]), ALU.mult)
                tt("pool", KHT, KK[pb], RR[pb], ALU.mult)
                tt("pool", GN[pb], SGT[pb], NG[:, :], ALU.mult)

            def p3(i):
                pb = i % 2
                for hh in range(2):
                    tr_b(pt[:, hh * 128:(hh + 1) * 128], KHT[:, hh * 128:(hh + 1) * 128], 128)
                vcopy('dve', KTK[pb], pt[:, 0:256])
                for hh in range(2):
                    mm(pA[:, hh * 128:(hh + 1) * 128], KHT[:, hh * 128:(hh + 1) * 128], QP[:, hh * 128:(hh + 1) * 128], True, True)
                tt("dve", ATM[pb].rearrange("p (h t) -> p h t", h=2), pA.rearrange("p (h t) -> p h t", h=2),
                   MASKA.unsqueeze(1).to_broadcast([128, 2, 128]), ALU.mult)

            def fin1a(nr, o_src):
                for hh in range(2):
                    act(JUNK[0:nr, :], o_src[hh][0:nr, :], AF.Square, accum=SSQ[0:nr, hh:hh + 1])
                rstd_from(SSQ[0:nr, :], RST[0:nr, :], nr, 1.0 / 128.0, RMS_EPS, RTM[0:nr, :])

            def fin1b(nr, o_src, gn):
                for hh in range(2):
                    stt(OG[0:nr, hh * 128:(hh + 1) * 128], o_src[hh][0:nr, :], RST[0:nr, hh:hh + 1],
                        gn[0:nr, hh * 128:(hh + 1) * 128], ALU.mult, ALU.mult)

            def fin2(nr, tile, first, last):
                fin2a(nr)
                fin2b(nr, tile, first, last)

            def fin2b(nr, tile, first, last):
                x_acc(tile, nr, yps[0:nr, :], first)
                if last:
                    ln_tile(tile, nr, 0)

            def fin2a(nr):
                for hh in range(2):
                    tr_b(pt[:, 256 + hh * 128: 256 + hh * 128 + nr], OG[0:nr, hh * 128:(hh + 1) * 128], nr)
                if nr == 128:
                    vcopy("dve", OGT, pt[:, 256:512])
                else:
                    vcopy("dve", OGT.rearrange("p (h t) -> p h t", h=2)[:, :, 0:nr], pt[:, 256:512].rearrange("p (h t) -> p h t", h=2)[:, :, 0:nr])
                for hh in range(2):
                    for half in range(2):
                        mm(yps[0:nr, half * 512:(half + 1) * 512], OGT[:, hh * 128: hh * 128 + nr], wout[:, hh, half * 512:(half + 1) * 512], hh == 0, hh == 1)

            def chain(i, jc):
                pb = i % 2
                for hh in range(2):
                    c = cur[0]
                    mm(POH[hh], QM[pb][:, hh * 256 + jc * 128: hh * 256 + jc * 128 + 128],
                       SB[c][:, hh * 128:(hh + 1) * 128], False, jc == NCK - 1)
                    mm(pdS[:, hh * 128:(hh + 1) * 128], KTK[pb][jc * CH:(jc + 1) * CH, hh * 128:(hh + 1) * 128],
                       VB[i % 3][jc * CH:(jc + 1) * CH, hh * 128:(hh + 1) * 128], True, True)
                    pc = PP[i % 3][:, hh * 128 + jc * CH + CH - 1: hh * 128 + jc * CH + CH]
                    stt(SST[:, hh * 128:(hh + 1) * 128], SST[:, hh * 128:(hh + 1) * 128], pc, pdS[:, hh * 128:(hh + 1) * 128], ALU.mult, ALU.add)
                    acopy(SB[1 - c][:, hh * 128:(hh + 1) * 128], SST[:, hh * 128:(hh + 1) * 128])
                cur[0] = 1 - cur[0]

            NTL = DEBUG.get('ntiles', 16)
            p0(0); p1(0)
            if NTL > 1:
                p0(1); p1(1)
            p2(0); p3(0)
            for t in range(NTL):
                pb = t % 2
                for hh in range(2):
                    mm(POH[hh], ATM[pb][:, hh * 128:(hh + 1) * 128], VB[t % 3][:, hh * 128:(hh + 1) * 128], True, False)
                chain(t, 0)
                if t >= 1:
                    fin2a(128)
                if t + 1 < NTL:
                    p2(t + 1)
                if t >= 1:
                    fin2b(128, t - 1, gi == 0, gi == 3)
                if t + 2 < NTL:
                    p0(t + 2)
                flush()
                for jc in range(1, NCK):
                    chain(t, jc)
                for hh in range(2):
                    acopy(OSBF[hh], POH[hh])
                fin1a(128, OSBF)
                if t + 1 < NTL:
                    p3(t + 1)
                if t + 2 < NTL:
                    p1(t + 2)
                fin1b(128, OSBF, GN[pb])
            fin2(128, NTL - 1, gi == 0, gi == 3)
            flush()
            for hh in range(2):
                dma_out(sp_o[j, 2 * gi + hh], SST[:, hh * 128:(hh + 1) * 128])

            if DEBUG.get('nosample'):
                continue
            tok0 = SEQ
            for sec in range(2):
                for hh in range(2):
                    for kc in range(8):
                        mm(pqf[:, sec * 32 + hh * 16: sec * 32 + hh * 16 + 16], win[:, kc, sec, hh * 128:(hh + 1) * 128],
                           XT3[:, kc, tok0:tok0 + NS], kc == 0, kc == 7)
            for sec in range(2):
                for kc in range(8):
                    mm(pvg[0:NS, sec * 256:(sec + 1) * 256], XT3[:, kc, tok0:tok0 + NS], win[:, kc, 2 + sec, :], kc == 0, kc == 7)
            act(TH[:, 0:32], pqf[:, 32:64], AF.Tanh, scale=0.5)
            act(SQ[0][:, 0:32], pqf[:, 0:32], AF.Silu)
            acopy(VB[0][0:NS, :], pvg[0:NS, 0:256])
            act(SGT[0][0:NS, :], pvg[0:NS, 256:512], AF.Silu)
            FSS = SCRF[:, 3760:3792]
            KKS = SCRF[:, 3792:3824]
            QSS = SCRF[:, 3824:3856]
            for hh in range(2):
                c = hcol[hh]
                ts("dve", FSS[:, hh * 16:(hh + 1) * 16], TH[:, hh * 16:(hh + 1) * 16], cB[:, c:c + 1], cA[:, c:c + 1], ALU.mult, ALU.add)
                ts("pool", KKS[:, hh * 16:(hh + 1) * 16], TH[:, hh * 16:(hh + 1) * 16], nB[:, c:c + 1], cB[:, c:c + 1], ALU.mult, ALU.add)
            ts("dve", QSS, SQ[0][:, 0:32], float(QSCALE), None, ALU.mult)
            tt("pool", GN[0][0:NS, :], SGT[0][0:NS, :], NG[0:NS, :], ALU.mult)
            for hh in range(2):
                tr_f(pA[0:NS, hh * 128:(hh + 1) * 128], KKS[:, hh * 16:(hh + 1) * 16], 128)
            DI = L1[:, 1536:1792]
            QD = SCRB[:, 4608:5120].rearrange("p (h b m) -> p h b m", h=2, b=NS)
            SNB = [SCRB[:, 4096 + k * 128: 4224 + k * 128] for k in range(4)]
            memset("dve", DI, 0.0)
            memset("dve", DI[:, 0:256:17], 1.0)
            for hh in range(2):
                tt("dve", QD[:, hh], QSS[:, hh * 16:(hh + 1) * 16].unsqueeze(2).to_broadcast([128, NS, NS]),
                   DI.rearrange("p (b m) -> p b m", b=NS), ALU.mult)
            po2 = ps[:, 2304:2560]
            order = [(hh, bh, b8) for hh in range(2) for bh in range(2) for b8 in range(8)]
            NSL = 8

            def ld(kk):
                hh_, bh_, b8_ = order[kk]
                dma_in(S0[kk % 8], st[j, bh_ * 8 + b8_, 2 * gi + hh_])

            def kv(kk):
                hh_, bh_, b8_ = order[kk]
                if b8_ == 0:
                    tt("dve", KD, pA[0:NS, hh_ * 128:(hh_ + 1) * 128].unsqueeze(1).to_broadcast([NS, 8, 128]),
                       IDF[0:NS, bh_ * 8:(bh_ + 1) * 8].unsqueeze(2).to_broadcast([NS, 8, 128]), ALU.mult)
                dsl = pdS[:, (kk % 2) * 128:(kk % 2) * 128 + 128]
                mm(dsl, KD[:, b8_, :], VB[0][0:NS, hh_ * 128:(hh_ + 1) * 128], True, True)
                b_ = bh_ * 8 + b8_
                stt(SN[kk % NSL], S0[kk % 8], FSS[:, hh_ * 16 + b_: hh_ * 16 + b_ + 1], dsl, ALU.mult, ALU.add)
                dma_out(ss_o[j, b_, 2 * gi + hh_], SN[kk % NSL], q=("act" if kk % 2 else "sp"))
                acopy(SNB[kk % 4], SN[kk % NSL])

            def omm(kk):
                hh_, bh_, b8_ = order[kk]
                b_ = bh_ * 8 + b8_
                mm(po2[0:NS, hh_ * 128:(hh_ + 1) * 128], QD[:, hh_, b_, :], SNB[kk % 4], b_ == 0, b_ == NS - 1)

            for kk in range(7):
                ld(kk)
            LAG = 3
            for kk in range(len(order) + LAG):
                if kk < len(order):
                    if kk + 7 < len(order):
                        ld(kk + 7)
                    kv(kk)
                if kk >= LAG:
                    omm(kk - LAG)
            osrc = [po2[:, 0:128], po2[:, 128:256]]
            fin1a(NS, osrc)
            fin1b(NS, osrc, GN[0])
            fin2(NS, 16, gi == 0, gi == 3)
            flush()
        prefetch_next_stage(si + 1, False)

    setup()
    for si, name in enumerate(stages):
        l = int(name[1])
        no_xt[0] = (si == len(stages) - 1)
        final_ln[0] = no_xt[0] and name[0] == "F"
        if name[0] == "A":
            a_stage(si, l)
        elif name[0] == "F":
            ffn_stage(si, l)
        else:
            b_stage(si, l)
    flush()
    for i in range(16):
        if i not in out_done:
            dma_out(yp[i * 128:(i + 1) * 128, :], X3[:, i, :])
    if 16 not in out_done:
        dma_out(ys[:, :], X3[0:NS, 16, :])
    run_sched(nc, S, [], es.enter_context)
    es.close()
    return nc


def _shared_layouts(inp):
    f = lambda a: np.ascontiguousarray(np.asarray(a, dtype=np.float32))
    a_w_in = f(inp["a_w_in"]); a_w_out = f(inp["a_w_out"])
    b_w_in = f(inp["b_w_in"]); b_w_out = f(inp["b_w_out"]); b_w_s = f(inp["b_w_s"]); b_bias_s = f(inp["b_bias_s"])
    ffn_w_in = f(inp["ffn_w_in"]); ffn_w_out = f(inp["ffn_w_out"])
    sh = {}
    sh["lnmg"] = f(inp["ln_mix_g"]); sh["lnmb"] = f(inp["ln_mix_b"])
    sh["lnfg"] = f(inp["ln_ffn_g"]); sh["lnfb"] = f(inp["ln_ffn_b"])
    sh["lbT"] = f(f(inp["a_lb_raw"]).reshape(4, 8, 128).transpose(2, 1, 0).reshape(128, 32))
    sh["a_win"] = f(a_w_in.reshape(2, 8, 128, 4, 4, 2, 128).transpose(0, 4, 2, 1, 3, 5, 6).reshape(2, 4, 128, 8192))
    sh["a_wout"] = f(a_w_out.reshape(2, 4, 2, 128, 1024).transpose(0, 1, 3, 2, 4).reshape(2, 4, 128, 2048))
    sh["a_ng"] = f(inp["a_norm_g"])
    sh["b_win"] = f(b_w_in.reshape(2, 8, 128, 2048).transpose(0, 2, 1, 3).reshape(2, 128, 8 * 2048))
    sh["b_wout"] = f(b_w_out.reshape(2, 8, 128, 1024).transpose(0, 2, 1, 3).reshape(2, 128, 8 * 1024))
    sh["b_lng"] = f(inp["b_ln_g"]); sh["b_lnb"] = f(inp["b_ln_b"])
    sh["b_wsT"] = f(b_w_s.transpose(0, 3, 1, 2).reshape(2, 128, 1024))
    sh["b_bias"] = f(b_bias_s.reshape(2, 1024))
    sh["b_w00"] = f(b_w_s[:, :, 0, 0]); sh["b_b0"] = f(b_bias_s[:, :, 0])
    sh["f_win"] = f(ffn_w_in.reshape(4, 8, 128, 2, NFC, 128).transpose(0, 2, 1, 4, 3, 5).reshape(4, 128, 8 * 2 * DFF))
    sh["f_wout"] = f(ffn_w_out.reshape(4, NFC, 128, 1024).transpose(0, 2, 1, 3).reshape(4, 128, NFC * 1024))
    idx = np.arange(128)
    maska = ((idx[:, None] // CH) == (idx[None, :] // CH)) & (idx[:, None] <= idx[None, :])
    tril = idx[:, None] <= idx[None, :]
    sh["cmask"] = f(np.concatenate([maska, tril], axis=1))
    sh["cident"] = f(np.eye(128))
    return sh


def _core_inputs(inp, sh, c):
    f = lambda a: np.ascontiguousarray(np.asarray(a, dtype=np.float32))
    m = dict(sh)
    m["xp"] = f(np.asarray(inp["x_prompt"])[c])
    m["xs"] = f(np.asarray(inp["x_sample"])[c * NS:(c + 1) * NS, 0, :])
    m["st"] = f(np.asarray(inp["state_hgrn"])[:, c * NS:(c + 1) * NS])
    return m


def run_cores(inp, cores, nstages=8):
    nc = build_program(nstages)
    sh = _shared_layouts(inp)
    in_maps = [_core_inputs(inp, sh, c) for c in cores]
    res = run_bass_kernel_spmd(nc, in_maps, core_ids=list(range(len(cores))))
    return res.results


def kernel(**inputs):
    res = run_cores(inputs, list(range(NCORES)), 8)
    y_prompt = np.zeros((NCORES, SEQ, D), np.float32)
    y_sample = np.zeros((NCORES * NS, 1, D), np.float32)
    sp = np.zeros((2, NCORES, 8, 128, 128), np.float32)
    ss = np.zeros((2, NCORES * NS, 8, 128, 128), np.float32)
    cvp = np.zeros((2, NCORES, 128, D), np.float32)
    cvs = np.zeros((2, NCORES * NS, 1, D), np.float32)
    for c, r in enumerate(res):
        y_prompt[c] = r["yp"]
        y_sample[c * NS:(c + 1) * NS, 0] = r["ys"]
        sp[:, c] = r["sp"]
        ss[:, c * NS:(c + 1) * NS] = r["ss"]
        cvp[:, c] = r["cvp"]
        cvs[:, c * NS:(c + 1) * NS, 0] = r["cvs"]
    return (y_prompt, y_sample, sp, ss, cvp, cvs)
```
